# Optimizing a Trainium2 kernel written in Bass

```python
import math, functools
import jax, jax.numpy as jnp
from jax import lax
import numpy as np

D_MODEL = 1024
BATCH = 16
SEQ = 2048
DEPTH = 2
DEC_BATCH = 128
DEC_SEQ = 1
PAST_LEN = 16384
PAGE_SIZE = 128

W_A = D_MODEL // 2
HEAD_SIZE_A = 64
H_A = W_A // HEAD_SIZE_A
DECAY_LORA = 64
A_LORA = 64
SHIFT_W = 3 * W_A + DECAY_LORA + A_LORA
GN_EPS = 64e-5
W_B = D_MODEL // 2
HEAD_DIM = 64
H_B = W_B // HEAD_DIM
KV_HEADS = 2
Q_PER_KV = H_B // KV_HEADS
KV_W = KV_HEADS * HEAD_DIM
WINDOW = 128
N_BUCKETS = 32
MAX_DISTANCE = 128
NORM_EPS = 1e-6
NEG_INF = -1e30
IN_SPLITS = (W_A, W_B, KV_W, KV_W, W_B, D_MODEL, D_MODEL)
IN_COLS = SHIFT_W + W_A + 2 * W_B + 2 * KV_W + 2 * D_MODEL

kernel_name = "hybrid_rwkv7_swa_gated_decoder_step"


def rms_norm(x, gain):
    xf = x.astype(jnp.float32)
    y = xf * lax.rsqrt(jnp.mean(xf * xf, axis=-1, keepdims=True) + NORM_EPS)
    return (y * gain.astype(jnp.float32)).astype(x.dtype)


def t5_bucket(dist):
    max_exact = N_BUCKETS // 2
    d = jnp.maximum(dist, 0)
    log_ratio = jnp.log(jnp.maximum(d, 1).astype(jnp.float32) / max_exact) / math.log(MAX_DISTANCE / max_exact)
    large = jnp.minimum(max_exact + (log_ratio * (N_BUCKETS - max_exact)).astype(jnp.int32), N_BUCKETS - 1)
    return jnp.where(d < max_exact, d, large)


def head_bias(dist, rel_bias):
    b = rel_bias[t5_bucket(dist)].astype(jnp.float32)
    return jnp.moveaxis(b, -1, 0).reshape(KV_HEADS, Q_PER_KV, *dist.shape)


def sink_attend(logits, sinks, v, eq):
    s = sinks.astype(jnp.float32).reshape(KV_HEADS, Q_PER_KV, 1, 1)
    m = jnp.maximum(jnp.max(logits, axis=-1, keepdims=True), s)
    p = jnp.exp(logits - m)
    probs = p / (jnp.sum(p, axis=-1, keepdims=True) + jnp.exp(s - m))
    return jnp.einsum(eq, probs.astype(v.dtype), v)


def swa_prompt(q, k, v, rel_bias, sinks):
    B, T = q.shape[:2]
    nb = T // WINDOW
    qb = q.reshape(B, nb, WINDOW, KV_HEADS, Q_PER_KV, HEAD_DIM)

    def band(t):
        tb = t.reshape(B, nb, WINDOW, KV_HEADS, HEAD_DIM)
        prev = jnp.concatenate([jnp.zeros_like(tb[:, :1]), tb[:, :-1]], axis=1)
        return jnp.concatenate([prev, tb], axis=2)

    kb, vb = band(k), band(v)
    logits = jnp.einsum('bnqgrd,bnkgd->bngrqk', qb, kb).astype(jnp.float32) * HEAD_DIM ** -0.5
    qi = jnp.arange(WINDOW)[:, None]
    kj = jnp.arange(2 * WINDOW)[None, :]
    dist = qi + WINDOW - kj
    band_ok = (dist >= 0) & (dist <= WINDOW)
    blk = jnp.arange(nb)[:, None, None]
    valid = band_ok[None] & ((blk > 0) | (kj >= WINDOW)[None])
    logits = jnp.where(valid[None, :, None, None], logits + head_bias(dist, rel_bias), NEG_INF)
    out = sink_attend(logits, sinks, vb, 'bngrqk,bnkgd->bnqgrd')
    return out.reshape(B, T, W_B), k[:, -WINDOW:], v[:, -WINDOW:]


def swa_sample(q, k, v, k_buf, v_buf, rel_bias, sinks):
    B, T = q.shape[:2]
    kc = jnp.concatenate([k_buf.astype(k.dtype), k], axis=1)
    vc = jnp.concatenate([v_buf.astype(v.dtype), v], axis=1)
    qg = q.reshape(B, T, KV_HEADS, Q_PER_KV, HEAD_DIM)
    logits = jnp.einsum('bqgrd,bkgd->bgrqk', qg, kc).astype(jnp.float32) * HEAD_DIM ** -0.5
    dist = jnp.arange(T)[:, None] + WINDOW - jnp.arange(WINDOW + T)[None, :]
    valid = (dist >= 0) & (dist <= WINDOW)
    logits = jnp.where(valid, logits + head_bias(dist, rel_bias), NEG_INF)
    out = sink_attend(logits, sinks, vc, 'bgrqk,bkgd->bqgrd')
    return out.reshape(B, T, W_B), kc[:, -WINDOW:], vc[:, -WINDOW:]


def rwkv_mix(u_sh, prev_row, s0, mu, w0, w_up, a0, a_up, k_k, k_a, r_k, lnx_g, lnx_b):
    B, T, _ = u_sh.shape
    f32 = jnp.float32
    u_prev = jnp.concatenate([prev_row[:, None, :].astype(u_sh.dtype), u_sh[:, :-1]], axis=1)
    xs = u_sh + (u_prev - u_sh) * mu
    r, k, v, wd, ad = jnp.split(xs, [W_A, 2 * W_A, 3 * W_A, 3 * W_A + DECAY_LORA], axis=-1)
    w_log = -jax.nn.softplus(-(w0 + jnp.tanh(wd) @ w_up).astype(f32)) - 0.5
    decay = jnp.exp(-jnp.exp(w_log))
    a = jax.nn.sigmoid((a0 + ad @ a_up).astype(f32))
    heads = lambda t: t.astype(f32).reshape(B, T, H_A, HEAD_SIZE_A)
    kk = heads(k * k_k)
    kk = kk / jnp.maximum(jnp.sqrt(jnp.sum(kk * kk, axis=-1, keepdims=True)), 1e-12)
    k_mod = k.astype(f32) * (1.0 + (a - 1.0) * k_a)
    r_h, k_h, v_h, w_h, a_h = heads(r), heads(k_mod), heads(v), heads(decay), heads(a)
    seq = lambda t: jnp.swapaxes(t, 0, 1)

    def step(S, inp):
        r_t, w_t, k_t, v_t, kk_t, a_t = inp
        sa = jnp.einsum('bhvk,bhk->bhv', S, -kk_t)
        S = (S * w_t[:, :, None, :] + sa[..., None] * (kk_t * a_t)[:, :, None, :]
             + v_t[..., None] * k_t[:, :, None, :])
        return S, jnp.einsum('bhvk,bhk->bhv', S, r_t)

    S_fin, o = lax.scan(step, s0.astype(f32), (seq(r_h), seq(w_h), seq(k_h), seq(v_h), seq(kk), seq(a_h)))
    o = seq(o)
    mean = jnp.mean(o, axis=-1, keepdims=True)
    var = jnp.mean((o - mean) ** 2, axis=-1, keepdims=True)
    o = ((o - mean) * lax.rsqrt(var + GN_EPS)).reshape(B, T, W_A) * lnx_g + lnx_b
    bonus = jnp.sum(r_h * k_h * r_k, axis=-1, keepdims=True) * v_h
    y = o + bonus.reshape(B, T, W_A)
    return y.astype(u_sh.dtype), S_fin, u_sh[:, -1]


def mixer_layer(x, c, shift0, wkv0, attn_fn, norm_g, w_ada, b_ada, w_in, mu_shift, w0, w_decay_up, a0,
                w_a_up, k_k, k_a, r_k, lnx_g, lnx_b, w_o_a, q_norm_g, k_norm_g, w_o_b, w_out):
    B, T, _ = x.shape
    mod = (jax.nn.silu(c) @ w_ada + b_ada)[:, None, :]
    shift, scale, gate = jnp.split(mod, 3, axis=-1)
    h = rms_norm(x, norm_g) * (1.0 + scale) + shift
    u = h @ w_in
    u_sh = u[..., :SHIFT_W]
    z_a, q, k, v, z_b, g_a, g_b = jnp.split(u[..., SHIFT_W:], np.cumsum(IN_SPLITS)[:-1].tolist(), axis=-1)
    o_a, wkv_new, shift_new = rwkv_mix(u_sh, shift0, wkv0, mu_shift, w0, w_decay_up, a0, w_a_up,
                                       k_k, k_a, r_k, lnx_g, lnx_b)
    y_a = o_a * jax.nn.silu(z_a)
    qh = rms_norm(q.reshape(B, T, H_B, HEAD_DIM), q_norm_g)
    kh = rms_norm(k.reshape(B, T, KV_HEADS, HEAD_DIM), k_norm_g)
    vh = v.reshape(B, T, KV_HEADS, HEAD_DIM)
    o_b, k_state, v_state = attn_fn(qh, kh, vh)
    y_b = o_b * jax.nn.silu(z_b)
    merged = jax.nn.sigmoid(g_a) * (y_a @ w_o_a) + jax.nn.sigmoid(g_b) * (y_b @ w_o_b)
    return x + gate * (merged @ w_out), wkv_new, shift_new, k_state, v_state


def setup_inputs(seed: int = 0) -> dict:
    key = jax.random.key(seed)
    ks = iter(jax.random.split(key, 40))
    n = lambda shape, s=1.0: s * jax.random.normal(next(ks), shape, jnp.float32)
    D = D_MODEL
    return {
        "x_prompt": n((BATCH, SEQ, D)),
        "x_sample": n((DEC_BATCH, DEC_SEQ, D)),
        "c_prompt": n((BATCH, D)),
        "c_sample": n((DEC_BATCH, D)),
        "state_wkv": n((DEPTH, DEC_BATCH, H_A, HEAD_SIZE_A, HEAD_SIZE_A), 0.3),
        "state_shift": n((DEPTH, DEC_BATCH, SHIFT_W)),
        "cache_k": n((DEPTH, DEC_BATCH, WINDOW, KV_HEADS, HEAD_DIM)),
        "cache_v": n((DEPTH, DEC_BATCH, WINDOW, KV_HEADS, HEAD_DIM)),
        "norm_g": 1.0 + n((DEPTH, D), 0.02),
        "w_ada": n((DEPTH, D, 3 * D), 0.5 * D ** -0.5),
        "b_ada": n((DEPTH, 3 * D), 0.02),
        "w_in": n((DEPTH, D, IN_COLS), D ** -0.5),
        "mu_shift": jax.random.uniform(next(ks), (DEPTH, SHIFT_W), jnp.float32),
        "w0": jax.random.uniform(next(ks), (DEPTH, W_A), jnp.float32, -6.0, 1.0),
        "w_decay_up": n((DEPTH, DECAY_LORA, W_A), 0.5 * DECAY_LORA ** -0.5),
        "a0": n((DEPTH, W_A), 0.1),
        "w_a_up": n((DEPTH, A_LORA, W_A), A_LORA ** -0.5),
        "k_k": 0.85 + n((DEPTH, W_A), 0.1),
        "k_a": 1.0 + n((DEPTH, W_A), 0.1),
        "r_k": n((DEPTH, H_A, HEAD_SIZE_A), 0.1),
        "lnx_g": 1.0 + n((DEPTH, W_A), 0.02),
        "lnx_b": n((DEPTH, W_A), 0.02),
        "w_o_a": n((DEPTH, W_A, D), W_A ** -0.5),
        "q_norm_g": 1.0 + n((DEPTH, HEAD_DIM), 0.02),
        "k_norm_g": 1.0 + n((DEPTH, HEAD_DIM), 0.02),
        "rel_bias": n((N_BUCKETS, H_B), 0.5),
        "sinks": n((DEPTH, H_B)),
        "w_o_b": n((DEPTH, W_B, D), W_B ** -0.5),
        "w_out": n((DEPTH, D, D), D ** -0.5),
    }


def reference(x_prompt, x_sample, c_prompt, c_sample, state_wkv, state_shift, cache_k, cache_v,
              norm_g, w_ada, b_ada, w_in, mu_shift, w0, w_decay_up, a0, w_a_up, k_k, k_a, r_k,
              lnx_g, lnx_b, w_o_a, q_norm_g, k_norm_g, rel_bias, sinks, w_o_b, w_out):
    xp, xs = x_prompt, x_sample
    wkv_p, shift_p, kw_p, vw_p, wkv_s, shift_s, kw_s, vw_s = [], [], [], [], [], [], [], []
    nb_p = x_prompt.shape[0]
    zero_shift = jnp.zeros((nb_p, SHIFT_W), x_prompt.dtype)
    zero_wkv = jnp.zeros((nb_p, H_A, HEAD_SIZE_A, HEAD_SIZE_A), jnp.float32)
    for l in range(DEPTH):
        lw = (norm_g[l], w_ada[l], b_ada[l], w_in[l], mu_shift[l], w0[l], w_decay_up[l], a0[l], w_a_up[l],
              k_k[l], k_a[l], r_k[l], lnx_g[l], lnx_b[l], w_o_a[l], q_norm_g[l], k_norm_g[l], w_o_b[l], w_out[l])
        attn_p = functools.partial(swa_prompt, rel_bias=rel_bias, sinks=sinks[l])
        xp, s_wkv, s_sh, s_k, s_v = mixer_layer(xp, c_prompt, zero_shift, zero_wkv, attn_p, *lw)
        wkv_p.append(s_wkv); shift_p.append(s_sh); kw_p.append(s_k); vw_p.append(s_v)
        attn_s = functools.partial(swa_sample, k_buf=cache_k[l], v_buf=cache_v[l], rel_bias=rel_bias, sinks=sinks[l])
        xs, s_wkv, s_sh, s_k, s_v = mixer_layer(xs, c_sample, state_shift[l], state_wkv[l], attn_s, *lw)
        wkv_s.append(s_wkv); shift_s.append(s_sh); kw_s.append(s_k); vw_s.append(s_v)
    y_prompt, y_sample = xp, xs
    return (y_prompt, y_sample, jnp.stack(wkv_p), jnp.stack(shift_p), jnp.stack(kw_p), jnp.stack(vw_p),
            jnp.stack(wkv_s), jnp.stack(shift_s), jnp.stack(kw_s), jnp.stack(vw_s))
```

```python
import numpy as np
import concourse.bass as bass
import concourse.mybir as mybir
from concourse.bass_utils import run_bass_kernel_spmd
from contextlib import ExitStack

F32 = mybir.dt.float32
BF16 = mybir.dt.bfloat16
AF = mybir.ActivationFunctionType
ALU = mybir.AluOpType
AX = mybir.AxisListType

NCORES = 8
D = 1024
SEQ = 2048
DEPTH = 2
NS = 16
NP = 2
TB = 256
NBLK = SEQ // TB
SHIFT_W = 1664
INC = 5504
WINC = INC + 256
C_R, C_K, C_V, C_WA, C_ZA, C_Q, C_KB, C_VB, C_ZB, C_GA, C_GB = 0, 512, 1024, 1536, 1664, 2176, 2688, 2816, 2944, 3456, 4480
C_KD = INC
NPC = 83
DC = 0.6065306597126334
NEGM = -30000.0
K_ID, K_MUS, K_MLS, K_MUI, K_SCAN, K_BO, K_ID2, K_SEL, K_ONE, K_END = 0, 128, 256, 384, 512, 768, 896, 960, 1216, 1344


class Res:
    __slots__ = ("w", "r")

    def __init__(self):
        self.w = {}
        self.r = {}


class Sched:
    ENG = ("tensor", "vector", "scalar", "gpsimd", "sync")

    def __init__(self, nc, ctx, ndma=8):
        self.nc = nc
        self.th = {e: [] for e in self.ENG}
        self.sem = {}
        self.cnt = {}
        self.known = {e: {} for e in self.ENG}
        for e in ("tensor", "vector", "scalar", "gpsimd"):
            self.sem[e] = ctx.enter_context(nc.semaphore("s_" + e))
            self.cnt[e] = 0
        self.dsem = {}
        self.dcnt = {}
        self.drr = {}
        for q in ("sync", "gpsimd"):
            self.dsem[q] = [ctx.enter_context(nc.semaphore(f"d_{q}{i}")) for i in range(ndma)]
            self.dcnt[q] = [0] * ndma
            self.drr[q] = 0
        self.same_sync = {"tensor": False, "vector": True, "scalar": True, "gpsimd": True, "sync": False}
        self.ctx = ctx
        self.last_tok = {}
        self.epoch = {e: 0 for e in self.sem}

    def _deps(self, e, reads, writes):
        need = {}
        for r in reads:
            for s, v in r.w.items():
                if need.get(s, 0) < v:
                    need[s] = v
        for w in writes:
            for d in (w.w, w.r):
                for s, v in d.items():
                    if need.get(s, 0) < v:
                        need[s] = v
        kn = self.known[e]
        out = []
        own = self.sem.get(e)
        for s, v in need.items():
            if s is own and not self.same_sync[e]:
                continue
            if kn.get(s, 0) < v:
                kn[s] = v
                out.append((s, v))
        return out

    def _commit(self, tok, reads, writes):
        s, v = tok
        for r in reads:
            r.r[s] = v
        for w in writes:
            if w.r:
                w.w = {s: v}
                w.r = {}
            else:
                w.w[s] = v

    def op(self, e, fn, reads=(), writes=()):
        _tick(e)
        waits = self._deps(e, reads, writes)
        if self.cnt[e] >= 20000:
            self.epoch[e] += 1
            self.sem[e] = self.ctx.enter_context(self.nc.semaphore(f"s_{e}_{self.epoch[e]}"))
            self.cnt[e] = 0
        self.cnt[e] += 1
        sem = self.sem[e]

        eng = getattr(self.nc, e)
        for s, v in waits:
            eng.wait_ge(s, v)
        fn(eng).then_inc(sem, 1)
        self.last_tok[e] = (sem, self.cnt[e])
        self._commit((sem, self.cnt[e]), reads, writes)

    def dma(self, q, out, in_, reads=(), writes=()):
        i = self.drr[q]
        self.drr[q] = (i + 1) % len(self.dsem[q])
        sem = self.dsem[q][i]
        prev = self.dcnt[q][i]
        waits = self._deps(q, reads, writes)
        kn = self.known[q]
        if prev and kn.get(sem, 0) < prev:
            kn[sem] = prev
            waits.append((sem, prev))
        self.dcnt[q][i] = prev + 16

        eng = getattr(self.nc, q)
        for s, v in waits:
            eng.wait_ge(s, v)
        eng.dma_start(out=out, in_=in_).then_inc(sem, 16)
        self._commit((sem, prev + 16), reads, writes)

    def barrier(self):
        toks = {}
        for e, c in self.cnt.items():
            if c:
                toks[self.sem[e]] = c
        for q in self.dsem:
            for i, s in enumerate(self.dsem[q]):
                if self.dcnt[q][i]:
                    toks[s] = self.dcnt[q][i]
        for e in self.ENG:
            kn = self.known[e]
            own = self.sem.get(e)
            waits = []
            for s, v in toks.items():
                if s is own:
                    continue
                if kn.get(s, 0) < v:
                    kn[s] = v
                    waits.append((s, v))

            eng = getattr(self.nc, e)
            for s, v in waits:
                eng.wait_ge(s, v)

    def emit(self):
        return
        with self.nc.Block() as block:
            for e in self.ENG:
                ths = self.th[e]
                if not ths:
                    continue

                def body(eng, ths=ths):
                    for t in ths:
                        t(eng)

                getattr(block, e)(body)


class StopBuild(Exception):
    pass


import os
MK_STOP = os.environ.get("MK_STOP", "")


MK_STOPN = int(os.environ.get("MK_STOPN", "0"))
_CNT = {"on": False, "n": 0}


def stop_at(tag):
    if MK_STOP == tag:
        raise StopBuild(tag)
    if MK_STOPN and tag == "xload":
        _CNT["on"] = True


def _tick(desc):
    if _CNT["on"]:
        _CNT["n"] += 1
        if _CNT["n"] == MK_STOPN:
            import traceback
            fr = traceback.extract_stack()
            print("STOPN last op:", desc, [f"{f.lineno}" for f in fr[-6:-1]])
        if _CNT["n"] > MK_STOPN:
            _CNT["on"] = False
            raise StopBuild(f"n={MK_STOPN}")


class T:
    __slots__ = ("t", "r")

    def __init__(self, t):
        self.t = t
        self.r = Res()

    def __getitem__(self, k):
        return self.t[k]


def build_nc():
    nc = bass.Bass("TRN2", target_bir_lowering=False)
    dt_in = lambda n, s: nc.dram_tensor(n, list(s), F32, kind="ExternalInput").ap()
    dt_out = lambda n, s: nc.dram_tensor(n, list(s), F32, kind="ExternalOutput").ap()
    xp = dt_in("xp", [NP, SEQ, D])
    xs_in = dt_in("xs", [NS, D])
    c_all = dt_in("c_all", [18, D])
    st_wkv = dt_in("st_wkv", [DEPTH, NS, 8, 64, 64])
    st_shift = dt_in("st_shift", [DEPTH, NS, SHIFT_W])
    ca_k = dt_in("ca_k", [DEPTH, NS, 128, 128])
    ca_v = dt_in("ca_v", [DEPTH, NS, 128, 128])
    w_ada = dt_in("w_ada", [DEPTH, D, 3 * D])
    w_in = dt_in("w_in", [DEPTH, D, INC])
    w_o_a = dt_in("w_o_a", [DEPTH, 512, D])
    w_o_b = dt_in("w_o_b", [DEPTH, 512, D])
    w_out = dt_in("w_out", [DEPTH, D, D])
    wlora = dt_in("wlora", [DEPTH, 128, 512])
    pcol_d = dt_in("pcol", [128, DEPTH, NPC])
    bgate_d = dt_in("bgate", [18, DEPTH, D])
    bm_d = dt_in("bm", [128, 4, 512])
    cst_d = dt_in("cst", [128, K_END])
    smp_d = dt_in("smpc", [128, 8 + DEPTH * 8 + 8])

    y_p = dt_out("y_p", [NP, SEQ, D])
    y_s = dt_out("y_s", [NS, D])
    o_wkv_p = dt_out("o_wkv_p", [DEPTH, NP, 8, 64, 64])
    o_sh_p = dt_out("o_sh_p", [DEPTH, NP, SHIFT_W])
    o_k_p = dt_out("o_k_p", [DEPTH, NP, 128, 128])
    o_v_p = dt_out("o_v_p", [DEPTH, NP, 128, 128])
    o_wkv_s = dt_out("o_wkv_s", [DEPTH, NS, 8, 64, 64])
    o_sh_s = dt_out("o_sh_s", [DEPTH, NS, SHIFT_W])
    o_k_s = dt_out("o_k_s", [DEPTH, NS, 128, 128])
    o_v_s = dt_out("o_v_s", [DEPTH, NS, 128, 128])
    xmid = nc.dram_tensor("xmid", [NP, SEQ, D], F32).ap()
    xs_mid = nc.dram_tensor("xs_mid", [NS, D], F32).ap()
    gate_d = nc.dram_tensor("gate_d", [DEPTH, 18, D], F32).ap()
    r_gd = [Res() for _ in range(DEPTH)]
    r_xsmid = Res()
    r_xmid = [[Res() for _ in range(NBLK)] for _ in range(NP)]
    out_res = []

    with ExitStack() as ctx:
        S = Sched(nc, ctx)
        off = [16512]
        uid = [0]

        def sb(shape, dt=F32):
            nb = int(np.prod(shape[1:])) * (4 if dt == F32 else 2)
            o = off[0]
            off[0] += (nb + 31) // 32 * 32
            assert off[0] <= 229344, f"SBUF overflow {off[0]}"
            uid[0] += 1
            return T(nc.alloc_sbuf_tensor_at(f"t{uid[0]}", list(shape), dt, offset=o))

        banks = []
        for i in range(8):
            b = T(nc.alloc_psum_tensor(f"pb{i}", [128, 512], F32))
            banks.append(b)
        prr = {"p": [0, [0, 1]], "c": [0, [2, 3, 4]], "m": [0, [5, 7]]}

        def pget(g):
            st = prr[g]
            b = banks[st[1][st[0] % len(st[1])]]
            st[0] += 1
            return b

        def V(fn, reads=(), writes=()):
            S.op("vector", fn, [x.r for x in reads], [x.r for x in writes])

        def A(fn, reads=(), writes=()):
            S.op("scalar", fn, [x.r for x in reads], [x.r for x in writes])

        def G(fn, reads=(), writes=()):
            S.op("gpsimd", fn, [x.r for x in reads], [x.r for x in writes])

        last_rg = [(0, 128)]

        def pe_rg(lhsT):
            rg = (lhsT.base_partition(), lhsT.partition_size())
            if rg != last_rg[0] and S.last_tok.get("tensor"):
                sem, val = S.last_tok["tensor"]
                if S.known["tensor"].get(sem, 0) < val:
                    S.known["tensor"][sem] = val
                    nc.tensor.wait_ge(sem, val)
            last_rg[0] = rg

        def MM(out, lhsT, rhs, start, stop, reads, writes):
            pe_rg(lhsT)
            S.op("tensor", lambda e: e.matmul(out, lhsT, rhs, start=start, stop=stop),
                 [x.r for x in reads], [x.r for x in writes])

        def TR(out, in_, ident, reads, writes):
            pe_rg(in_)
            S.op("tensor", lambda e: e.transpose(out, in_, ident), [x.r for x in reads], [x.r for x in writes])

        def DMA(out, in_, reads=(), writes=(), q="sync"):
            S.dma(q, out, in_, [x if isinstance(x, Res) else x.r for x in reads],
                  [x if isinstance(x, Res) else x.r for x in writes])

        Win = sb([128, 8, WINC], BF16)
        Woa = sb([128, 4, D], BF16)
        Wob = sb([128, 4, D], BF16)
        Wout = sb([128, 8, D], BF16)
        Wlo = sb([128, 512], BF16)
        cst = sb([128, K_END])
        pcol = sb([128, DEPTH, NPC])
        dcol = sb([128, 16])
        smpc = sb([128, 8 + DEPTH * 8 + 8])
        identb = sb([128, 128], BF16)
        bo_b = sb([128, 128], BF16)
        bo64_b = sb([128, 128], BF16)
        ones_b = sb([128, 64], BF16)
        modT = sb([128, 16, 18])
        AT = sb([128, 8, 18])
        gate_bc = sb([128, D])
        scT = sb([128, 8, 18])
        ssq = sb([128, 8])
        base_off = off[0]

        identf = cst.t[:, K_ID:K_ID + 128]
        m_us = cst.t[:, K_MUS:K_MUS + 128]
        m_ls = cst.t[:, K_MLS:K_MLS + 128]
        m_ui = cst.t[:, K_MUI:K_MUI + 128]
        scanm = cst.t[:, K_SCAN:K_SCAN + 256]
        bo_f = cst.t[:, K_BO:K_BO + 128]
        ident2 = cst.t[:, K_ID2:K_ID2 + 64]
        sel = cst.t[0:18, K_SEL:K_SEL + 256]

        DMA(cst.t[:], cst_d, writes=[cst])
        DMA(pcol.t[:], pcol_d, writes=[pcol])
        DMA(smpc.t[:], smp_d, writes=[smpc])
        V(lambda e: e.tensor_copy(identb.t[:], identf), [cst], [identb])
        V(lambda e: e.tensor_copy(bo_b.t[:], bo_f), [cst], [bo_b])
        V(lambda e: e.tensor_scalar(out=bo64_b.t[:], in0=bo_f, scalar1=1.0 / 64, scalar2=None, op0=ALU.mult), [cst], [bo64_b])
        V(lambda e: e.memset(ones_b.t[:], 1.0), [], [ones_b])

        def pc(l, j, n=1):
            return pcol.t[:, l, j:j + n]

        def _layers():
            for l in range(DEPTH):
                off[0] = base_off
                def load_weights(lw, part):
                    wv = w_in[lw].rearrange("(kc p) n -> p kc n", p=128)
                    if part == "win":
                        for kc in range(8):
                            for q4 in range(4):
                                c0 = q4 * 1376
                                DMA(Win.t[:, kc, c0:c0 + 1376], wv[:, kc, c0:c0 + 1376], writes=[Win], q="gpsimd")
                        for g in range(2):
                            for cp in range(2):
                                DMA(Win.t[:, :, C_KD + g * 128 + cp * 64:C_KD + g * 128 + cp * 64 + 64],
                                    wv[:, :, C_KB + g * 64:C_KB + g * 64 + 64], writes=[Win], q="gpsimd")
                    else:
                        DMA(Woa.t[:], w_o_a[lw].rearrange("(c p) n -> p c n", p=128), writes=[Woa], q="gpsimd")
                        DMA(Wob.t[:], w_o_b[lw].rearrange("(c p) n -> p c n", p=128), writes=[Wob], q="gpsimd")
                        DMA(Wout.t[:], w_out[lw].rearrange("(c p) n -> p c n", p=128), writes=[Wout], q="gpsimd")
                        DMA(Wlo.t[:], wlora[lw], writes=[Wlo], q="gpsimd")

                if l == 0:
                    load_weights(0, "win")
                    load_weights(0, "rest")
                V(lambda e: e.tensor_scalar(out=dcol.t[:, 0:4], in0=pc(l, 33, 4), scalar1=-1.0, scalar2=1.0, op0=ALU.mult, op1=ALU.add), [pcol], [dcol])
                V(lambda e: e.tensor_scalar(out=dcol.t[:, 4:5], in0=pc(l, 49), scalar1=0.125, scalar2=None, op0=ALU.mult), [pcol], [dcol])
                A(lambda e: e.activation(out=dcol.t[:, 5:9], in_=pc(l, 51, 4), func=AF.Exp), [pcol], [dcol])

                call = sb([18, D])
                gate_tok = sb([18, D])
                bg = sb([18, D])
                stage = [sb([128, 8, 512]), sb([128, 8, 512])]
                DMA(call.t[:], c_all, writes=[call])
                DMA(bg.t[:], bgate_d[:, l, :], writes=[bg])
                for kc in range(8):
                    pb = pget("m")
                    TR(pb.t[:, 0:18], call.t[0:18, kc * 128:(kc + 1) * 128], identf[0:18, 0:18], [call, cst], [pb])
                    A(lambda e, pb=pb, kc=kc: e.activation(out=scT.t[:, kc, :], in_=pb.t[:, 0:18], func=AF.Silu), [pb], [scT])
                wav = w_ada[l].rearrange("(kc p) n -> p kc n", p=128)
                for pcs in range(6):
                    stg = stage[pcs % 2]
                    for kh in range(2):
                        DMA(stg.t[:, kh * 4:(kh + 1) * 4, :], wav[:, kh * 4:(kh + 1) * 4, pcs * 512:(pcs + 1) * 512], writes=[stg])
                    if pcs < 4:
                        for j in range(4):
                            ncnk = pcs * 4 + j
                            pb = pget("m")
                            for kc in range(8):
                                MM(pb.t[:, 0:18], stg.t[:, kc, j * 128:(j + 1) * 128], scT.t[:, kc, :], kc == 0, kc == 7, [stg, scT], [pb])
                            A(lambda e, pb=pb, ncnk=ncnk: e.activation(out=modT.t[:, ncnk, :], in_=pb.t[:, 0:18], func=AF.Identity,
                                                                       bias=pc(l, 55 + ncnk), scale=1.0), [pb, pcol], [modT])
                    else:
                        hf = pcs - 4
                        pb = pget("m")
                        for kc in range(8):
                            MM(pb.t[0:18, :], scT.t[:, kc, :], stg.t[:, kc, :], kc == 0, kc == 7, [stg, scT], [pb])
                        V(lambda e, pb=pb, hf=hf: e.tensor_tensor(out=gate_tok.t[:, hf * 512:(hf + 1) * 512], in0=pb.t[0:18, :],
                                                                  in1=bg.t[:, hf * 512:(hf + 1) * 512], op=ALU.add), [pb, bg], [gate_tok])
                DMA(gate_d[l], gate_tok.t[:], reads=[gate_tok], writes=[r_gd[l]])
                for kc in range(8):
                    V(lambda e, kc=kc: e.tensor_scalar(out=AT.t[:, kc, :], in0=modT.t[:, 8 + kc, :], scalar1=1.0, scalar2=pc(l, kc),
                                                       op0=ALU.add, op1=ALU.mult), [modT, pcol], [AT])
                S.barrier()
                off[0] = base_off
                stop_at(f"mod{l}")

                hT = sb([128, 8, TB], BF16)
                xts = [sb([128, D]), sb([128, D])]
                yaT = sb([128, 4, TB], BF16)
                ybT = sb([128, 4, TB], BF16)
                mT = sb([128, 8, TB], BF16)
                qnT = sb([128, 4, TB], BF16)
                knT = sb([128, 2, 128 + TB], BF16)
                vbtok = sb([128, 3, 128], BF16)
                opA = sb([128, TB], BF16)
                opB = sb([128, TB], BF16)
                opK = sb([128, TB], BF16)
                opR = sb([128, TB], BF16)
                vtok = sb([128, 2, 128], BF16)
                btok = sb([128, 2, 128], BF16)
                ktok = sb([128, 2, 128], BF16)
                Hst = [sb([128, 64]) for _ in range(4)]
                Hbf = [sb([128, 128], BF16), sb([128, 128], BF16)]
                for hb_ in Hbf:
                    V(lambda e, hb_=hb_: e.memset(hb_.t[:], 0.0), [], [hb_])
                carry = sb([128, 13])
                Rraw = [sb([128, TB + 1]), sb([128, TB + 1])]
                BMt = sb([128, 512])
                Qm = sb([128, 2, 128])
                Nm = sb([128, 2, 128])
                Arb = sb([128, 2, 128], BF16)
                Aak = sb([128, 2, 128], BF16)
                Ark = sb([128, 2, 128], BF16)
                Qp = [sb([128, 2, 128]), sb([128, 2, 128])]
                Np_ = [sb([128, 2, 128]), sb([128, 2, 128])]
                Yb = [sb([128, 2, 128]), sb([128, 2, 128])]
                Ybf = sb([128, 2, 128], BF16)
                CS = [dict(Qm=Qm, Nm=Nm, Arb=Arb, Aak=Aak, Ark=Ark, Qp=Qp, Np=Np_, Yb=Yb, Ybf=Ybf),
                      dict(Qm=sb([128, 2, 128]), Nm=sb([128, 2, 128]), Arb=sb([128, 2, 128], BF16), Aak=sb([128, 2, 128], BF16),
                           Ark=sb([128, 2, 128], BF16), Qp=[sb([128, 2, 128]), sb([128, 2, 128])], Np=[sb([128, 2, 128]), sb([128, 2, 128])],
                           Yb=[sb([128, 2, 128]), sb([128, 2, 128])], Ybf=sb([128, 2, 128], BF16))]
                if os.environ.get("MK_VERBOSE"):
                    print("prompt-phase SBUF end", off[0], "limit 229344")
                W0sb = sb([128, 128], BF16)
                Usb = sb([128, 128], BF16)
                Fp = [sb([128, TB]) for _ in range(8)]
                Bp = [sb([128, TB], BF16) for _ in range(4)]
                dXR, dXK, dXV, dEP, dBON, dKM, dAG = [sb([128, TB]) for _ in range(7)]
                dTWD = sb([128, TB], BF16)
                frr = [0]
                brr = [0]

                def fget():
                    t = Fp[frr[0] % len(Fp)]
                    frr[0] += 1
                    return t

                def bget():
                    t = Bp[brr[0] % len(Bp)]
                    brr[0] += 1
                    return t


                def proj(col0, ncols=128):
                    pb = pget("p")
                    for kc in range(8):
                        MM(pb.t[0:ncols, 0:TB], Win.t[:, kc, col0:col0 + ncols], hT.t[:, kc, :], kc == 0, kc == 7, [Win, hT], [pb])
                    return pb

                xpf = [False]
                for s in range(NP):
                    DMA(gate_bc.t[:], gate_d[l, 16 + s:17 + s, :].partition_broadcast(128), reads=[r_gd[l]], writes=[gate_bc])
                    for hp in range(4):
                        V(lambda e, hp=hp: e.memset(Hst[hp].t[:], 0.0), [], [Hst[hp]])
                    V(lambda e: e.memset(carry.t[:], 0.0), [], [carry])
                    V(lambda e: e.memset(knT.t[:, :, 0:128], 0.0), [], [knT])
                    V(lambda e: e.memset(vbtok.t[:, 0, :], 0.0), [], [vbtok])
                    for blk in range(NBLK):
                        t0 = blk * TB
                        xsrc = xp if l == 0 else xmid
                        xdst = xmid if l == 0 else y_p
                        rd_x = [r_xmid[s][blk]] if l == 1 else []
                        last = blk == NBLK - 1
                        for ti in range(TB // 128):
                            xt = xts[ti % 2]
                            if ti == 0 and xpf[0]:
                                xpf[0] = False
                            else:
                                DMA(xt.t[:], xsrc[s, t0 + ti * 128:t0 + (ti + 1) * 128, :], reads=rd_x, writes=[xt])
                            jk = fget()
                            V(lambda e: e.memset(ssq.t[:, 4:8], 0.0), [], [ssq])
                            for hh in range(4):
                                A(lambda e, hh=hh, jk=jk: e.activation(out=jk.t[:, :], in_=xt.t[:, hh * 256:(hh + 1) * 256], func=AF.Square,
                                                                       accum_out=ssq.t[:, 4 + hh:5 + hh]), [xt, ssq], [jk, ssq])
                            V(lambda e: e.tensor_reduce(out=ssq.t[:, 0:1], in_=ssq.t[:, 4:8], axis=AX.X, op=ALU.add), [ssq], [ssq])
                            A(lambda e: e.activation(out=ssq.t[:, 1:2], in_=ssq.t[:, 0:1], func=AF.Ln, bias=1e-6, scale=1.0 / D), [ssq], [ssq])
                            A(lambda e: e.activation(out=ssq.t[:, 2:3], in_=ssq.t[:, 1:2], func=AF.Exp, scale=-0.5), [ssq], [ssq])
                            V(lambda e: e.tensor_scalar(out=xt.t[:], in0=xt.t[:], scalar1=ssq.t[:, 2:3], scalar2=None, op0=ALU.mult), [xt, ssq], [xt])
                            for kh in range(2):
                                pb = pget("m")
                                for k4 in range(4):
                                    kc = kh * 4 + k4
                                    TR(pb.t[:, k4 * 128:(k4 + 1) * 128], xt.t[:, kc * 128:(kc + 1) * 128], identf, [xt, cst], [pb])
                                for k4 in range(4):
                                    kc = kh * 4 + k4
                                    A(lambda e, pb=pb, k4=k4, kc=kc, ti=ti: e.activation(
                                        out=hT.t[:, kc, ti * 128:(ti + 1) * 128], in_=pb.t[:, k4 * 128:(k4 + 1) * 128], func=AF.Identity,
                                        bias=modT.t[:, kc, 16 + s:17 + s], scale=AT.t[:, kc, 16 + s:17 + s]), [pb, modT, AT], [hT])

                        def shifted(cidx, col0, xo):
                            pb = proj(col0)
                            R = Rraw[cidx % 2]
                            A(lambda e: e.copy(R.t[:, 1:TB + 1], pb.t[:, 0:TB]), [pb], [R])
                            G(lambda e: e.tensor_copy(R.t[:, 0:1], carry.t[:, cidx:cidx + 1]), [carry], [R])
                            G(lambda e: e.tensor_copy(carry.t[:, cidx:cidx + 1], R.t[:, TB:TB + 1]), [R], [carry])
                            dd = fget()
                            G(lambda e: e.tensor_tensor(out=dd.t[:], in0=R.t[:, 0:TB], in1=R.t[:, 1:TB + 1], op=ALU.subtract), [R], [dd])
                            V(lambda e: e.scalar_tensor_tensor(out=xo.t[:], in0=dd.t[:], scalar=pc(l, 8 + cidx), in1=R.t[:, 1:TB + 1],
                                                               op0=ALU.mult, op1=ALU.add), [dd, R, pcol], [xo])
                            return xo

                        stop_at('xload')
                        wa = shifted(12, C_WA, fget())
                        twd = dTWD
                        A(lambda e: e.activation(out=twd.t[0:64, :], in_=wa.t[0:64, :], func=AF.Tanh), [wa], [twd])
                        V(lambda e: e.tensor_copy(twd.t[64:128, :], wa.t[64:128, :]), [wa], [twd])
                        for hp in range(4):
                            xr = shifted(hp, C_R + hp * 128, dXR)
                            xk = shifted(4 + hp, C_K + hp * 128, dXK)
                            xv = shifted(8 + hp, C_V + hp * 128, dXV)
                            pw = pget("m")
                            MM(pw.t[:, 0:TB], Wlo.t[0:64, hp * 128:(hp + 1) * 128], twd.t[0:64, :], True, True, [Wlo, twd], [pw])
                            MM(pw.t[:, TB:2 * TB], Wlo.t[64:128, hp * 128:(hp + 1) * 128], twd.t[64:128, :], True, True, [Wlo, twd], [pw])
                            sg = fget()
                            ag = dAG
                            A(lambda e: e.activation(out=sg.t[:], in_=pw.t[:, 0:TB], func=AF.Sigmoid, bias=pc(l, 21 + hp), scale=1.0), [pw, pcol], [sg])
                            A(lambda e: e.activation(out=ag.t[:], in_=pw.t[:, TB:2 * TB], func=AF.Sigmoid, bias=pc(l, 25 + hp), scale=1.0), [pw, pcol], [ag])
                            Gc = fget()
                            V(lambda e: e.tensor_tensor_scan(out=Gc.t[:], data0=scanm, data1=sg.t[:], initial=0.0, op0=ALU.mult, op1=ALU.add), [cst, sg], [Gc])
                            Gx = fget()
                            G(lambda e: e.tensor_tensor(out=Gx.t[:], in0=Gc.t[:], in1=sg.t[:], op=ALU.subtract), [Gc, sg], [Gx])
                            Ep = dEP
                            Em = fget()
                            Ex = fget()
                            A(lambda e: e.activation(out=Ep.t[:], in_=Gc.t[:], func=AF.Exp, scale=-DC), [Gc], [Ep])
                            A(lambda e: e.activation(out=Em.t[:], in_=Gc.t[:], func=AF.Exp, scale=DC), [Gc], [Em])
                            A(lambda e: e.activation(out=Ex.t[:], in_=Gx.t[:], func=AF.Exp, scale=-DC), [Gx], [Ex])
                            kr = fget()
                            V(lambda e: e.tensor_scalar(out=kr.t[:], in0=xk.t[:], scalar1=pc(l, 29 + hp), scalar2=None, op0=ALU.mult), [xk, pcol], [kr])
                            sq = bget()
                            A(lambda e: e.activation(out=sq.t[:], in_=kr.t[:], func=AF.Square), [kr], [sq])
                            pn = pget("m")
                            MM(pn.t[:, 0:TB], bo_b.t[:], sq.t[:], True, True, [bo_b, sq], [pn])
                            nr = fget()
                            A(lambda e: e.activation(out=nr.t[:], in_=pn.t[:, 0:TB], func=AF.Ln, bias=1e-24, scale=1.0), [pn], [nr])
                            A(lambda e: e.activation(out=nr.t[:], in_=nr.t[:], func=AF.Exp, scale=-0.5), [nr], [nr])
                            kk = fget()
                            V(lambda e: e.tensor_tensor(out=kk.t[:], in0=kr.t[:], in1=nr.t[:], op=ALU.mult), [kr, nr], [kk])
                            km = dKM
                            V(lambda e: e.tensor_scalar(out=km.t[:], in0=ag.t[:], scalar1=pc(l, 33 + hp), scalar2=dcol.t[:, hp:hp + 1],
                                                        op0=ALU.mult, op1=ALU.add), [ag, pcol, dcol], [km])
                            G(lambda e: e.tensor_tensor(out=km.t[:], in0=km.t[:], in1=xk.t[:], op=ALU.mult), [km, xk], [km])
                            V(lambda e: e.scalar_tensor_tensor(out=opA.t[:], in0=kk.t[:], scalar=-1.0, in1=Ex.t[:], op0=ALU.mult, op1=ALU.mult), [kk, Ex], [opA])
                            kka = fget()
                            G(lambda e: e.tensor_tensor(out=kka.t[:], in0=kk.t[:], in1=ag.t[:], op=ALU.mult), [kk, ag], [kka])
                            V(lambda e: e.tensor_tensor(out=opB.t[:], in0=kka.t[:], in1=Em.t[:], op=ALU.mult), [kka, Em], [opB])
                            V(lambda e: e.tensor_tensor(out=opK.t[:], in0=km.t[:], in1=Em.t[:], op=ALU.mult), [km, Em], [opK])
                            V(lambda e: e.tensor_tensor(out=opR.t[:], in0=xr.t[:], in1=Ep.t[:], op=ALU.mult), [xr, Ep], [opR])
                            rk = bget()
                            V(lambda e: e.scalar_tensor_tensor(out=rk.t[:], in0=xr.t[:], scalar=pc(l, 37 + hp), in1=km.t[:], op0=ALU.mult, op1=ALU.mult), [xr, km, pcol], [rk])
                            pbn = pget("m")
                            MM(pbn.t[:, 0:TB], bo_b.t[:], rk.t[:], True, True, [bo_b, rk], [pbn])
                            bon = dBON
                            V(lambda e: e.tensor_tensor(out=bon.t[:], in0=pbn.t[:, 0:TB], in1=xv.t[:], op=ALU.mult), [pbn, xv], [bon])
                            vb16 = bget()
                            A(lambda e: e.copy(vb16.t[:], xv.t[:]), [xv], [vb16])
                            for (src, dst) in ((vb16, vtok), (opB, btok), (opK, ktok)):
                                pt = pget("m")
                                ptb = pt.t[:].bitcast(BF16)
                                for c in range(2):
                                    TR(ptb[:, c * 128:(c + 1) * 128], src.t[:, c * 128:(c + 1) * 128], identb.t[:], [src, identb], [pt])
                                A(lambda e, ptb=ptb, dst=dst: e.copy(dst.t[:].rearrange("p c t -> p (c t)"), ptb[:, 0:256]), [pt], [dst])
                            stop_at('r1')
                            H = Hst[hp]
                            pO = banks[6]
                            for c in range(2):
                                cs = slice(c * 128, (c + 1) * 128)
                                B = CS[c]
                                Qm, Nm, Arb, Aak, Ark, Yb = B["Qm"], B["Nm"], B["Arb"], B["Aak"], B["Ark"], B["Yb"]
                                pA_ = pget("c")
                                pB_ = pget("c")
                                pC_ = pget("c")
                                for h2 in range(2):
                                    ps_ = slice(h2 * 64, (h2 + 1) * 64)
                                    hs = slice(h2 * 128, (h2 + 1) * 128)
                                    MM(pA_.t[:, hs], opB.t[ps_, cs], opA.t[ps_, cs], True, True, [opA, opB], [pA_])
                                    MM(pA_.t[:, 256 + h2 * 128:256 + (h2 + 1) * 128], opA.t[ps_, cs], opB.t[ps_, cs], True, True, [opA, opB], [pA_])
                                    MM(pB_.t[:, hs], opB.t[ps_, cs], opR.t[ps_, cs], True, True, [opB, opR], [pB_])
                                    MM(pB_.t[:, 256 + h2 * 128:256 + (h2 + 1) * 128], opK.t[ps_, cs], opA.t[ps_, cs], True, True, [opK, opA], [pB_])
                                    MM(pC_.t[:, h2 * 128:(h2 + 1) * 128], opK.t[ps_, cs], opR.t[ps_, cs], True, True, [opK, opR], [pC_])
                                bc2 = lambda m: m.unsqueeze(1).to_broadcast([128, 2, 128])
                                v3 = lambda ap: ap.rearrange("p (h t) -> p h t", h=2)
                                V(lambda e: e.tensor_tensor(out=Qm.t[:], in0=v3(pA_.t[:, 0:256]), in1=bc2(m_us), op=ALU.mult), [pA_, cst], [Qm])
                                V(lambda e: e.tensor_tensor(out=Nm.t[:], in0=v3(pA_.t[:, 256:512]), in1=bc2(m_ls), op=ALU.mult), [pA_, cst], [Nm])
                                V(lambda e: e.tensor_tensor(out=Arb.t[:], in0=v3(pB_.t[:, 0:256]), in1=bc2(m_ui), op=ALU.mult), [pB_, cst], [Arb])
                                V(lambda e: e.tensor_tensor(out=Aak.t[:], in0=v3(pB_.t[:, 256:512]), in1=bc2(m_us), op=ALU.mult), [pB_, cst], [Aak])
                                V(lambda e: e.tensor_tensor(out=Ark.t[:], in0=v3(pC_.t[:, 0:256]), in1=bc2(m_ui), op=ALU.mult), [pC_, cst], [Ark])
                                G(lambda e: e.tensor_tensor(out=Yb[0].t[:], in0=Qm.t[:], in1=bc2(identf), op=ALU.add), [Qm, cst], [Yb[0]])
                                B["state"] = (Qm, Nm, Yb[0])
                            for lev in range(1, 7):
                                pLs = []
                                for c in range(2):
                                    B = CS[c]
                                    qc, ncur, yc = B["state"]
                                    pL = pget("c")
                                    pLs.append(pL)
                                    for h2 in range(2):
                                        MM(pL.t[:, 256 + h2 * 128:256 + (h2 + 1) * 128], qc.t[:, h2, :], ncur.t[:, h2, :], True, True, [qc, ncur], [pL])
                                        if lev < 6:
                                            MM(pL.t[:, h2 * 128:(h2 + 1) * 128], ncur.t[:, h2, :], qc.t[:, h2, :], True, True, [qc, ncur], [pL])
                                news = []
                                for c in range(2):
                                    B = CS[c]
                                    pL = pLs[c]
                                    nq = B["Qp"][lev % 2]
                                    nn = B["Np"][lev % 2]
                                    if lev < 6:
                                        A(lambda e, pL=pL, nq=nq: e.copy(nq.t[:].rearrange("p h t -> p (h t)"), pL.t[:, 0:256]), [pL], [nq])
                                    A(lambda e, pL=pL, nn=nn: e.copy(nn.t[:].rearrange("p h t -> p (h t)"), pL.t[:, 256:512]), [pL], [nn])
                                    news.append((nq, nn))
                                pYs = []
                                for c in range(2):
                                    B = CS[c]
                                    yc = B["state"][2]
                                    nn = news[c][1]
                                    pY = pget("c")
                                    pYs.append(pY)
                                    for h2 in range(2):
                                        MM(pY.t[:, h2 * 128:(h2 + 1) * 128], nn.t[:, h2, :], yc.t[:, h2, :], True, True, [nn, yc], [pY])
                                for c in range(2):
                                    B = CS[c]
                                    yc = B["state"][2]
                                    pY = pYs[c]
                                    ny = B["Yb"][lev % 2] if lev < 6 else B["Ybf"]
                                    V(lambda e, pY=pY, ny=ny, yc=yc: e.tensor_tensor(out=ny.t[:], in0=v3(pY.t[:, 0:256]), in1=yc.t[:], op=ALU.add), [pY, yc], [ny])
                                    B["state"] = (news[c][0], news[c][1], ny)
                            for c in range(2):
                                cs = slice(c * 128, (c + 1) * 128)
                                B = CS[c]
                                Arb, Aak, Ark = B["Arb"], B["Aak"], B["Ark"]
                                yc = B["state"][2]
                                hb = Hbf[c % 2]
                                A(lambda e, hb=hb: e.copy(hb.t[0:64, 0:64], H.t[0:64, :]), [H], [hb])
                                A(lambda e, hb=hb: e.copy(hb.t[64:128, 64:128], H.t[64:128, :]), [H], [hb])
                                pW = pget("c")
                                for h2 in range(2):
                                    ps_ = slice(h2 * 64, (h2 + 1) * 64)
                                    MM(pW.t[:, h2 * 64:(h2 + 1) * 64], opA.t[:, cs], hb.t[:, h2 * 64:(h2 + 1) * 64], True, False, [opA, hb], [pW])
                                    MM(pW.t[:, h2 * 64:(h2 + 1) * 64], Aak.t[:, h2, :], vtok.t[:, c, h2 * 64:(h2 + 1) * 64], False, True, [Aak, vtok], [pW])
                                A(lambda e, pW=pW: e.copy(W0sb.t[:], pW.t[:, 0:128]), [pW], [W0sb])
                                pU = pget("c")
                                for h2 in range(2):
                                    MM(pU.t[:, h2 * 64:(h2 + 1) * 64], yc.t[:, h2, :], W0sb.t[:, h2 * 64:(h2 + 1) * 64], True, True, [yc, W0sb], [pU])
                                A(lambda e, pU=pU: e.copy(Usb.t[:], pU.t[:, 0:128]), [pU], [Usb])
                                for h2 in range(2):
                                    ps_ = slice(h2 * 64, (h2 + 1) * 64)
                                    MM(pO.t[ps_, cs], hb.t[:, h2 * 64:(h2 + 1) * 64], opR.t[:, cs], True, False, [hb, opR], [pO])
                                    MM(pO.t[ps_, cs], Usb.t[:, h2 * 64:(h2 + 1) * 64], Arb.t[:, h2, :], False, False, [Usb, Arb], [pO])
                                    MM(pO.t[ps_, cs], vtok.t[:, c, h2 * 64:(h2 + 1) * 64], Ark.t[:, h2, :], False, True, [vtok, Ark], [pO])
                                pH = pget("c")
                                for h2 in range(2):
                                    ps_ = slice(h2 * 64, (h2 + 1) * 64)
                                    MM(pH.t[ps_, 0:64], btok.t[:, c, ps_], Usb.t[:, ps_], True, False, [btok, Usb], [pH])
                                    MM(pH.t[ps_, 0:64], ktok.t[:, c, ps_], vtok.t[:, c, ps_], False, True, [ktok, vtok], [pH])
                                V(lambda e, pH=pH: e.tensor_tensor(out=H.t[:], in0=pH.t[:, 0:64], in1=H.t[:], op=ALU.add), [pH, H], [H])
                                gcol = Ep.t[:, c * 128 + 127:c * 128 + 128]
                                V(lambda e, gcol=gcol: e.tensor_scalar(out=H.t[:], in0=H.t[:], scalar1=gcol, scalar2=None, op0=ALU.mult), [H, Ep], [H])
                            if last:
                                pt = pget("m")
                                TR(pt.t[0:64, 0:128], H.t[:, :], identf, [H, cst], [pt])
                                ho = fget()
                                A(lambda e: e.copy(ho.t[0:64, 0:128], pt.t[0:64, 0:128]), [pt], [ho])
                                DMA(o_wkv_p[l, s, 2 * hp:2 * hp + 2].rearrange("h v k -> v h k"),
                                    ho.t[0:64, 0:128].rearrange("p (h k) -> p h k", h=2), reads=[ho], writes=[])
                            stop_at('r4')
                            Osb = fget()
                            A(lambda e: e.copy(Osb.t[:], pO.t[:, 0:TB]), [pO], [Osb])
                            Ob = bget()
                            V(lambda e: e.tensor_copy(Ob.t[:], pO.t[:, 0:TB]), [pO], [Ob])
                            pm = pget("m")
                            MM(pm.t[:, 0:TB], bo64_b.t[:], Ob.t[:], True, True, [bo64_b, Ob], [pm])
                            dd = fget()
                            V(lambda e: e.tensor_tensor(out=dd.t[:], in0=Osb.t[:], in1=pm.t[:, 0:TB], op=ALU.subtract), [Osb, pm], [dd])
                            dq = bget()
                            A(lambda e: e.activation(out=dq.t[:], in_=dd.t[:], func=AF.Square), [dd], [dq])
                            pv_ = pget("m")
                            MM(pv_.t[:, 0:TB], bo64_b.t[:], dq.t[:], True, True, [bo64_b, dq], [pv_])
                            sd = fget()
                            A(lambda e: e.activation(out=sd.t[:], in_=pv_.t[:, 0:TB], func=AF.Ln, bias=64e-5, scale=1.0), [pv_], [sd])
                            A(lambda e: e.activation(out=sd.t[:], in_=sd.t[:], func=AF.Exp, scale=-0.5), [sd], [sd])
                            G(lambda e: e.tensor_tensor(out=dd.t[:], in0=dd.t[:], in1=sd.t[:], op=ALU.mult), [dd, sd], [dd])
                            G(lambda e: e.tensor_scalar(out=dd.t[:], in0=dd.t[:], scalar1=pc(l, 41 + hp), scalar2=pc(l, 45 + hp), op0=ALU.mult, op1=ALU.add), [dd, pcol], [dd])
                            G(lambda e: e.tensor_tensor(out=dd.t[:], in0=dd.t[:], in1=bon.t[:], op=ALU.add), [dd, bon], [dd])
                            pz = proj(C_ZA + hp * 128)
                            zs = fget()
                            A(lambda e: e.activation(out=zs.t[:], in_=pz.t[:, 0:TB], func=AF.Silu), [pz], [zs])
                            V(lambda e: e.tensor_tensor(out=yaT.t[:, hp, :], in0=dd.t[:], in1=zs.t[:], op=ALU.mult), [dd, zs], [yaT])

                        stop_at('rwkv')
                        stop_at(f'rwkv{blk}')
                        def qknorm(pb, gcol, outs):
                            sq = bget()
                            A(lambda e: e.activation(out=sq.t[:], in_=pb.t[:, 0:TB], func=AF.Square), [pb], [sq])
                            pmm = pget("m")
                            MM(pmm.t[:, 0:TB], bo64_b.t[:], sq.t[:], True, True, [bo64_b, sq], [pmm])
                            rs = fget()
                            A(lambda e: e.activation(out=rs.t[:], in_=pmm.t[:, 0:TB], func=AF.Ln, bias=1e-6, scale=1.0), [pmm], [rs])
                            A(lambda e: e.activation(out=rs.t[:], in_=rs.t[:], func=AF.Exp, scale=-0.5), [rs], [rs])
                            for (oap, ot, prt) in outs:
                                V(lambda e, oap=oap, prt=prt: e.scalar_tensor_tensor(out=oap, in0=pb.t[prt, 0:TB], scalar=gcol[prt, :], in1=rs.t[prt, :],
                                                                                     op0=ALU.mult, op1=ALU.mult), [pb, rs, pcol, dcol], [ot])

                        kf = fget()
                        allp = slice(0, 128)
                        for g in range(2):
                            pb = proj(C_KD + g * 128)
                            outs = [(knT.t[:, g, 128:128 + TB], knT, allp)]
                            if last:
                                outs.append((kf.t[g * 64:(g + 1) * 64, 0:128], kf, slice(g * 64, (g + 1) * 64)))
                            if last:
                                qknorm(pb, pc(l, 50), [outs[0]])
                                sqf = bget()
                                A(lambda e, pb=pb, sqf=sqf: e.activation(out=sqf.t[:, 0:128], in_=pb.t[:, TB - 128:TB], func=AF.Square), [pb], [sqf])
                                pmm = pget("m")
                                MM(pmm.t[:, 0:128], bo64_b.t[:], sqf.t[:, 0:128], True, True, [bo64_b, sqf], [pmm])
                                rsf = fget()
                                A(lambda e, pmm=pmm, rsf=rsf: e.activation(out=rsf.t[:, 0:128], in_=pmm.t[:, 0:128], func=AF.Ln, bias=1e-6, scale=1.0), [pmm], [rsf])
                                A(lambda e, rsf=rsf: e.activation(out=rsf.t[:, 0:128], in_=rsf.t[:, 0:128], func=AF.Exp, scale=-0.5), [rsf], [rsf])
                                gs = slice(g * 64, (g + 1) * 64)
                                V(lambda e, pb=pb, rsf=rsf, gs=gs: e.scalar_tensor_tensor(out=kf.t[gs, 0:128], in0=pb.t[gs, TB - 128:TB], scalar=pc(l, 50)[gs, :],
                                                                                         in1=rsf.t[gs, 0:128], op0=ALU.mult, op1=ALU.mult), [pb, rsf, pcol], [kf])
                            else:
                                qknorm(pb, pc(l, 50), outs)
                        if last:
                            pt = pget("m")
                            TR(pt.t[:, 0:128], kf.t[:, 0:128], identf, [kf, cst], [pt])
                            ko = fget()
                            A(lambda e, pt=pt, ko=ko: e.copy(ko.t[:, 0:128], pt.t[:, 0:128]), [pt], [ko])
                            DMA(o_k_p[l, s], ko.t[:, 0:128], reads=[ko], writes=[])
                            out_res.append(ko.r)
                        stop_at(f'k{blk}')
                        for hp in range(4):
                            pb = proj(C_Q + hp * 128)
                            qknorm(pb, dcol.t[:, 4:5], [(qnT.t[:, hp, :], qnT, allp)])
                        pb = proj(C_VB)
                        vf = fget()
                        A(lambda e: e.copy(vf.t[:], pb.t[:, 0:TB]), [pb], [vf])
                        pt = pget("m")
                        for ti in range(2):
                            TR(pt.t[:, ti * 128:(ti + 1) * 128], vf.t[:, ti * 128:(ti + 1) * 128], identf, [vf, cst], [pt])
                        A(lambda e, pt=pt: e.copy(vbtok.t[:, 1:3, :].rearrange("p c t -> p (c t)"), pt.t[:, 0:256]), [pt], [vbtok])
                        if last:
                            vo = fget()
                            A(lambda e, pt=pt, vo=vo: e.copy(vo.t[:, 0:128], pt.t[:, 128:256]), [pt], [vo])
                            DMA(o_v_p[l, s], vo.t[:, 0:128], reads=[vo], writes=[])
                            out_res.append(vo.r)
                        stop_at(f'v{blk}')
                        for hp in range(4):
                            g = hp // 2
                            DMA(BMt.t[:], bm_d[:, hp, :], writes=[BMt])
                            pOb = banks[6]
                            pDn = banks[7]
                            def _scores(ti_):
                                first_ = (blk == 0 and ti_ == 0)
                                psc_ = pget("c")
                                for h2 in range(2):
                                    ps_ = slice(h2 * 64, (h2 + 1) * 64)
                                    for hf in ([1] if first_ else [0, 1]):
                                        kcols = slice((ti_ + hf) * 128, (ti_ + hf + 1) * 128)
                                        MM(psc_.t[:, (h2 * 2 + hf) * 128:(h2 * 2 + hf + 1) * 128], knT.t[ps_, g, kcols], qnT.t[ps_, hp, ti_ * 128:(ti_ + 1) * 128],
                                           True, True, [knT, qnT], [psc_])
                                return psc_

                            nxt_psc = _scores(0)
                            for ti in range(2):
                                first = (blk == 0 and ti == 0)
                                halves = [1] if first else [0, 1]
                                psc = nxt_psc
                                if ti + 1 < 2:
                                    nxt_psc = _scores(ti + 1)
                                sc = fget()
                                scb = fget()
                                Pt = bget()
                                Pb = bget()
                                for h2 in range(2):
                                    dst = sc if h2 == 0 else scb
                                    pdst = Pt if h2 == 0 else Pb
                                    lo = 128 if first else 0
                                    V(lambda e, dst=dst, h2=h2, lo=lo, psc=psc: e.tensor_tensor(out=dst.t[:, lo:256], in0=psc.t[:, h2 * 256 + lo:(h2 + 1) * 256],
                                                                                               in1=BMt.t[:, h2 * 256 + lo:(h2 + 1) * 256], op=ALU.add), [psc, BMt], [dst])
                                    A(lambda e, dst=dst, pdst=pdst, lo=lo: e.activation(out=pdst.t[:, lo:256], in_=dst.t[:, lo:256], func=AF.Exp), [dst], [pdst])
                                    ps_ = slice(h2 * 64, (h2 + 1) * 64)
                                    oc = slice(ti * 128, (ti + 1) * 128)
                                    for i, hf in enumerate(halves):
                                        st, sp = i == 0, i == len(halves) - 1
                                        MM(pOb.t[ps_, oc], vbtok.t[:, ti + hf, g * 64:(g + 1) * 64], pdst.t[:, hf * 128:(hf + 1) * 128], st, sp, [vbtok, pdst], [pOb])
                                    for i, hf in enumerate(halves):
                                        st, sp = i == 0, i == len(halves) - 1
                                        MM(pDn.t[ps_, oc], ones_b.t[:, :], pdst.t[:, hf * 128:(hf + 1) * 128], st, sp, [ones_b, pdst], [pDn])
                            dn = fget()
                            V(lambda e, pDn=pDn, dn=dn: e.tensor_scalar(out=dn.t[:], in0=pDn.t[:, 0:TB], scalar1=dcol.t[:, 5 + hp:6 + hp], scalar2=None, op0=ALU.add), [pDn, dcol], [dn])
                            V(lambda e, dn=dn: e.reciprocal(dn.t[:], dn.t[:]), [dn], [dn])
                            V(lambda e, pOb=pOb, dn=dn: e.tensor_tensor(out=dn.t[:], in0=pOb.t[:, 0:TB], in1=dn.t[:], op=ALU.mult), [pOb, dn], [dn])
                            pz = proj(C_ZB + hp * 128)
                            zs = fget()
                            A(lambda e, pz=pz, zs=zs: e.activation(out=zs.t[:], in_=pz.t[:, 0:TB], func=AF.Silu), [pz], [zs])
                            V(lambda e, dn=dn, zs=zs, hp=hp: e.tensor_tensor(out=ybT.t[:, hp, :], in0=dn.t[:], in1=zs.t[:], op=ALU.mult), [dn, zs], [ybT])
                        G(lambda e: e.tensor_copy(knT.t[:, :, 0:128], knT.t[:, :, TB:TB + 128]), [knT], [knT])
                        G(lambda e: e.tensor_copy(vbtok.t[:, 0, :], vbtok.t[:, 2, :]), [vbtok], [vbtok])

                        stop_at('attn')
                        stop_at(f'attn{blk}')
                        for n in range(8):
                            pa = pget("c")
                            pbb = pget("c")
                            for c in range(4):
                                MM(pa.t[:, 0:TB], Woa.t[:, c, n * 128:(n + 1) * 128], yaT.t[:, c, :], c == 0, c == 3, [Woa, yaT], [pa])
                            for c in range(4):
                                MM(pbb.t[:, 0:TB], Wob.t[:, c, n * 128:(n + 1) * 128], ybT.t[:, c, :], c == 0, c == 3, [Wob, ybT], [pbb])
                            pga = proj(C_GA + n * 128)
                            sga = fget()
                            A(lambda e, pga=pga, sga=sga: e.activation(out=sga.t[:], in_=pga.t[:, 0:TB], func=AF.Sigmoid), [pga], [sga])
                            pgb = proj(C_GB + n * 128)
                            sgb = fget()
                            A(lambda e, pgb=pgb, sgb=sgb: e.activation(out=sgb.t[:], in_=pgb.t[:, 0:TB], func=AF.Sigmoid), [pgb], [sgb])
                            V(lambda e, pa=pa, sga=sga: e.tensor_tensor(out=sga.t[:], in0=pa.t[:, 0:TB], in1=sga.t[:], op=ALU.mult), [pa, sga], [sga])
                            V(lambda e, pbb=pbb, sgb=sgb: e.tensor_tensor(out=sgb.t[:], in0=pbb.t[:, 0:TB], in1=sgb.t[:], op=ALU.mult), [pbb, sgb], [sgb])
                            G(lambda e, sga=sga, sgb=sgb, n=n: e.tensor_tensor(out=mT.t[:, n, :], in0=sga.t[:], in1=sgb.t[:], op=ALU.add), [sga, sgb], [mT])

                        stop_at('merge')
                        stop_at(f'merge{blk}')
                        for ti in range(TB // 128):
                            xt = xts[ti % 2]
                            DMA(xt.t[:], xsrc[s, t0 + ti * 128:t0 + (ti + 1) * 128, :], reads=rd_x, writes=[xt])
                            for hf in range(2):
                                pf = pget("p")
                                for kc in range(8):
                                    MM(pf.t[:, :], mT.t[:, kc, ti * 128:(ti + 1) * 128], Wout.t[:, kc, hf * 512:(hf + 1) * 512], kc == 0, kc == 7, [mT, Wout], [pf])
                                o1 = fget()
                                o2 = fget()
                                hsl = slice(hf * 512, (hf + 1) * 512)
                                for q2, ot in enumerate((o1, o2)):
                                    cs2 = slice(hf * 512 + q2 * 256, hf * 512 + (q2 + 1) * 256)
                                    V(lambda e, pf=pf, ot=ot, q2=q2, cs2=cs2: e.tensor_tensor(out=ot.t[:], in0=pf.t[:, q2 * 256:(q2 + 1) * 256], in1=gate_bc.t[:, cs2], op=ALU.mult), [pf, gate_bc], [ot])
                                    G(lambda e, ot=ot, cs2=cs2: e.tensor_tensor(out=ot.t[:], in0=ot.t[:], in1=xt.t[:, cs2], op=ALU.add), [ot, xt], [ot])
                                    wr = [r_xmid[s][blk]] if l == 0 else []
                                    DMA(xdst[s, t0 + ti * 128:t0 + (ti + 1) * 128, cs2], ot.t[:], reads=[ot], writes=wr)
                                    if l == 1:
                                        out_res.append(ot.r)
                            if ti == 0 and blk + 1 < NBLK:
                                rdn = [r_xmid[s][blk + 1]] if l == 1 else []
                                DMA(xts[0].t[:], xsrc[s, t0 + TB:t0 + TB + 128, :], reads=rdn, writes=[xts[0]])
                                xpf[0] = True
                        if last:
                            pt = pget("m")
                            TR(pt.t[0:13, 0:128], carry.t[:, 0:13], identf, [carry, cst], [pt])
                            so = fget()
                            A(lambda e: e.copy(so.t[0:13, 0:128], pt.t[0:13, 0:128]), [pt], [so])
                            DMA(o_sh_p[l, s].rearrange("(c p) -> c p", p=128), so.t[0:13, 0:128], reads=[so], writes=[])
                        stop_at(f'blk{blk}')
                S.barrier()
                off[0] = base_off
                stop_at(f"prompt{l}")

                xn = sb([NS, D])
                gts = sb([NS, D])
                DMA(gts.t[:], gate_d[l, 0:NS, :], reads=[r_gd[l]], writes=[gts])
                xs_res = sb([NS, D])
                DMA(xs_res.t[:], xs_in if l == 0 else xs_mid, reads=[r_xsmid], writes=[xs_res])
                hTs = sb([128, 8, NS], BF16)
                us = sb([128, 43, NS])
                ua = sb([64, 20, NS])
                prevT = sb([NS, SHIFT_W])
                xsh = sb([128, 13, NS])
                sho = prevT
                Xv = sb([128, 5, 4, NS])
                vT = sb([128, 4, NS])
                bonS = sb([128, 4, NS])
                S0 = sb([128, 8, 4, 64])
                Dx = sb([128, 5, 512])
                tmpS = [sb([128, 512]), sb([128, 512])]
                sa = sb([128, 2, 4])
                oS = sb([128, NS, 4])
                cK = sb([128, NS, 128])
                cVap = S0.t[:].rearrange("p a b c -> p (a b c)").rearrange("p (b n) -> p b n", n=128)
                st = [sb([128, 4 * NS]) for _ in range(8)]
                h0s = sb([128, 8 * NS])
                sb16 = [sb([128, 4 * NS], BF16) for _ in range(3)]
                yaS = sb([128, 4, NS], BF16)
                ybS = sb([128, 4, NS], BF16)
                yb64 = sb([64, 8, NS], BF16)
                mS = sb([128, 8, NS], BF16)
                qn64 = sb([64, 10, NS])
                Dq = Dx.t[0:64, 0:4, :]
                scS = sb([128, NS, 8])
                pS = sb([128, NS, 8])
                at = [sb([64, 8, NS]) for _ in range(6)]
                kvo = sb([32, 2, 64])

                DMA(cK.t[:], ca_k[l].rearrange("b p n -> p b n"), writes=[cK])
                DMA(prevT.t[:], st_shift[l], writes=[prevT])
                V(lambda e: e.memset(ssq.t[0:NS, 0:1], 0.0), [], [ssq])
                A(lambda e: e.activation(out=xn.t[:], in_=xs_res.t[:], func=AF.Square, accum_out=ssq.t[0:NS, 0:1]), [xs_res, ssq], [xn, ssq])
                A(lambda e: e.activation(out=ssq.t[0:NS, 1:2], in_=ssq.t[0:NS, 0:1], func=AF.Sqrt, bias=1e-6, scale=1.0 / D), [ssq], [ssq])
                V(lambda e: e.reciprocal(ssq.t[0:NS, 2:3], ssq.t[0:NS, 1:2]), [ssq], [ssq])
                V(lambda e: e.tensor_scalar(out=xn.t[:], in0=xs_res.t[:], scalar1=ssq.t[0:NS, 2:3], scalar2=None, op0=ALU.mult), [xs_res, ssq], [xn])
                pb = pget("m")
                for kc in range(8):
                    TR(pb.t[:, kc * NS:(kc + 1) * NS], xn.t[0:NS, kc * 128:(kc + 1) * 128], identf[0:NS, 0:NS], [xn, cst], [pb])
                h0 = h0s
                v8 = lambda ap: ap.rearrange("p (c b) -> p c b", b=NS)
                V(lambda e: e.tensor_tensor(out=v8(h0.t[:, 0:8 * NS]), in0=v8(pb.t[:, 0:8 * NS]), in1=AT.t[:, :, 0:NS], op=ALU.mult), [pb, AT], [h0])
                V(lambda e: e.tensor_tensor(out=hTs.t[:], in0=v8(h0.t[:, 0:8 * NS]), in1=modT.t[:, 0:8, 0:NS], op=ALU.add), [h0, modT], [hTs])
                for grp in (list(range(0, 17)), list(range(27, 43))):
                    pb = pget("p")
                    for i, j in enumerate(grp):
                        for kc in range(8):
                            MM(pb.t[:, i * NS:(i + 1) * NS], Win.t[:, kc, j * 128:(j + 1) * 128], hTs.t[:, kc, :], kc == 0, kc == 7, [Win, hTs], [pb])
                    A(lambda e, pb=pb, grp=grp: e.copy(us.t[:, grp[0]:grp[-1] + 1, :].rearrange("p c b -> p (c b)"), pb.t[:, 0:len(grp) * NS]), [pb], [us])
                pb = pget("p")
                cols64 = [C_Q + h * 64 for h in range(8)] + [C_KB, C_KB + 64, C_VB, C_VB + 64] + [C_ZB + h * 64 for h in range(8)]
                for i, c0 in enumerate(cols64):
                    for kc in range(8):
                        MM(pb.t[0:64, i * NS:(i + 1) * NS], Win.t[:, kc, c0:c0 + 64], hTs.t[:, kc, :], kc == 0, kc == 7, [Win, hTs], [pb])
                A(lambda e, pb=pb: e.copy(ua.t[:].rearrange("p c b -> p (c b)"), pb.t[0:64, 0:20 * NS]), [pb], [ua])
                if l + 1 < DEPTH:
                    load_weights(l + 1, "win")
                pb = pget("m")
                for j in range(13):
                    TR(pb.t[:, j * NS:(j + 1) * NS], prevT.t[0:NS, j * 128:(j + 1) * 128], identf[0:NS, 0:NS], [prevT, cst], [pb])
                v13 = lambda ap: ap.rearrange("p (c b) -> p c b", b=NS)
                V(lambda e: e.tensor_tensor(out=xsh.t[:], in0=v13(pb.t[:, 0:13 * NS]), in1=us.t[:, 0:13, :], op=ALU.subtract), [pb, us], [xsh])
                V(lambda e: e.tensor_tensor(out=xsh.t[:], in0=xsh.t[:], in1=pcol.t[:, l, 8:21].unsqueeze(2).to_broadcast([128, 13, NS]), op=ALU.mult), [xsh, pcol], [xsh])
                V(lambda e: e.tensor_tensor(out=xsh.t[:], in0=xsh.t[:], in1=us.t[:, 0:13, :], op=ALU.add), [xsh, us], [xsh])
                for q4 in range(4):
                    n = 4 if q4 < 3 else 1
                    pb = pget("m")
                    for i in range(n):
                        j = q4 * 4 + i
                        TR(pb.t[0:NS, i * 128:(i + 1) * 128], us.t[:, j, :], identf, [us, cst], [pb])
                    A(lambda e, pb=pb, q4=q4, n=n: e.copy(sho.t[:, q4 * 512:q4 * 512 + n * 128], pb.t[0:NS, 0:n * 128]), [pb], [sho])
                DMA(o_sh_s[l], sho.t[:], reads=[sho], writes=[])
                out_res.append(sho.r)
                tw = sb16[0]
                A(lambda e: e.activation(out=tw.t[0:64, 0:NS], in_=xsh.t[0:64, 12, :], func=AF.Tanh), [xsh], [tw])
                V(lambda e: e.tensor_copy(tw.t[64:128, 0:NS], xsh.t[64:128, 12, :]), [xsh], [tw])
                pw = pget("m")
                for hp in range(4):
                    MM(pw.t[:, hp * NS:(hp + 1) * NS], Wlo.t[0:64, hp * 128:(hp + 1) * 128], tw.t[0:64, 0:NS], True, True, [Wlo, tw], [pw])
                    MM(pw.t[:, 64 + hp * NS:64 + (hp + 1) * NS], Wlo.t[64:128, hp * 128:(hp + 1) * 128], tw.t[64:128, 0:NS], True, True, [Wlo, tw], [pw])
                v4 = lambda ap: ap.rearrange("p (c b) -> p c b", b=NS)
                colb = lambda j0: pcol.t[:, l, j0:j0 + 4].unsqueeze(2).to_broadcast([128, 4, NS])
                sgS, agS, krS, kkS, kmS, t5, t6, t7 = st
                V(lambda e: e.tensor_tensor(out=v4(sgS.t[:]), in0=v4(pw.t[:, 0:64]), in1=colb(21), op=ALU.add), [pw, pcol], [sgS])
                V(lambda e: e.tensor_tensor(out=v4(agS.t[:]), in0=v4(pw.t[:, 64:128]), in1=colb(25), op=ALU.add), [pw, pcol], [agS])
                A(lambda e: e.activation(out=sgS.t[:], in_=sgS.t[:], func=AF.Sigmoid), [sgS], [sgS])
                A(lambda e: e.activation(out=agS.t[:], in_=agS.t[:], func=AF.Sigmoid), [agS], [agS])
                A(lambda e: e.activation(out=Xv.t[:, 1, :, :], in_=v4(sgS.t[:]), func=AF.Exp, scale=-DC), [sgS], [Xv])
                V(lambda e: e.tensor_tensor(out=v4(krS.t[:]), in0=xsh.t[:, 4:8, :], in1=colb(29), op=ALU.mult), [xsh, pcol], [krS])
                V(lambda e: e.tensor_tensor(out=t5.t[:], in0=krS.t[:], in1=krS.t[:], op=ALU.mult), [krS], [t5])
                pn = pget("m")
                MM(pn.t[:, 0:64], bo_f, t5.t[:], True, True, [cst, t5], [pn])
                A(lambda e: e.activation(out=t6.t[:], in_=pn.t[:, 0:64], func=AF.Sqrt), [pn], [t6])
                V(lambda e: e.tensor_scalar(out=t6.t[:], in0=t6.t[:], scalar1=1e-12, scalar2=None, op0=ALU.max), [t6], [t6])
                V(lambda e: e.reciprocal(t6.t[:], t6.t[:]), [t6], [t6])
                V(lambda e: e.tensor_tensor(out=kkS.t[:], in0=krS.t[:], in1=t6.t[:], op=ALU.mult), [krS, t6], [kkS])
                V(lambda e: e.tensor_scalar(out=Xv.t[:, 0, :, :], in0=v4(kkS.t[:]), scalar1=-1.0, scalar2=None, op0=ALU.mult), [kkS], [Xv])
                V(lambda e: e.tensor_tensor(out=Xv.t[:, 2, :, :], in0=v4(kkS.t[:]), in1=v4(agS.t[:]), op=ALU.mult), [kkS, agS], [Xv])
                V(lambda e: e.tensor_tensor(out=v4(kmS.t[:]), in0=v4(agS.t[:]), in1=colb(33), op=ALU.mult), [agS, pcol], [kmS])
                V(lambda e: e.tensor_tensor(out=v4(kmS.t[:]), in0=v4(kmS.t[:]), in1=dcol.t[:, 0:4].unsqueeze(2).to_broadcast([128, 4, NS]), op=ALU.add), [kmS, dcol], [kmS])
                V(lambda e: e.tensor_tensor(out=Xv.t[:, 3, :, :], in0=v4(kmS.t[:]), in1=xsh.t[:, 4:8, :], op=ALU.mult), [kmS, xsh], [Xv])
                V(lambda e: e.tensor_copy(Xv.t[:, 4, :, :], xsh.t[:, 0:4, :]), [xsh], [Xv])
                V(lambda e: e.tensor_copy(vT.t[:], xsh.t[:, 8:12, :]), [xsh], [vT])
                V(lambda e: e.tensor_tensor(out=v4(t5.t[:]), in0=xsh.t[:, 0:4, :], in1=colb(37), op=ALU.mult), [xsh, pcol], [t5])
                V(lambda e: e.tensor_tensor(out=v4(t5.t[:]), in0=v4(t5.t[:]), in1=Xv.t[:, 3, :, :], op=ALU.mult), [t5, Xv], [t5])
                pbn = pget("m")
                MM(pbn.t[:, 0:64], bo_f, t5.t[:], True, True, [cst, t5], [pbn])
                V(lambda e: e.tensor_tensor(out=bonS.t[:], in0=v4(pbn.t[:, 0:64]), in1=vT.t[:], op=ALU.mult), [pbn, vT], [bonS])
                for bp in range(NS // 2):
                    b0 = bp * 2
                    bl = b0 % 8
                    if bl == 0:
                        DMA(S0.t[:], st_wkv[l, b0:b0 + 8].rearrange("b (hp h2) v k -> (h2 v) b hp k", h2=2), writes=[S0])
                    for i in range(5):
                        V(lambda e, i=i, b0=b0: e.tensor_tensor(
                            out=Dx.t[:, i, :].rearrange("p (b hp k) -> p b hp k", b=2, hp=4),
                            in0=ident2.unsqueeze(1).unsqueeze(1).to_broadcast([128, 2, 4, 64]),
                            in1=Xv.t[:, i, :, b0:b0 + 2].rearrange("p hp b -> p b hp").unsqueeze(3).to_broadcast([128, 2, 4, 64]),
                            op=ALU.mult), [cst, Xv], [Dx])
                    pbs = []
                    for i in range(5):
                        pbk = banks[i]
                        MM(pbk.t[:, :], bo_f, Dx.t[:, i, :], True, True, [cst, Dx], [pbk])
                        pbs.append(pbk)
                    Sv = S0.t[:, bl:bl + 2, :, :].rearrange("p b hp k -> p (b hp k)")
                    t1, t2 = tmpS
                    V(lambda e, Sv=Sv: e.tensor_tensor(out=t1.t[:], in0=Sv, in1=pbs[0].t[:, :], op=ALU.mult), [S0, pbs[0]], [t1])
                    V(lambda e: e.tensor_reduce(out=sa.t[:].rearrange("p a b -> p (a b)"),
                                                in_=t1.t[:].rearrange("p (g k) -> p g k", k=64), axis=AX.X, op=ALU.add), [t1], [sa])
                    V(lambda e, Sv=Sv: e.tensor_tensor(out=Sv, in0=Sv, in1=pbs[1].t[:, :], op=ALU.mult), [S0, pbs[1]], [S0])
                    V(lambda e: e.tensor_tensor(out=t1.t[:].rearrange("p (g k) -> p g k", k=64), in0=pbs[2].t[:, :].rearrange("p (g k) -> p g k", k=64),
                                                in1=sa.t[:].rearrange("p a b -> p (a b)").unsqueeze(2).to_broadcast([128, 8, 64]), op=ALU.mult), [pbs[2], sa], [t1])
                    V(lambda e, Sv=Sv: e.tensor_tensor(out=Sv, in0=Sv, in1=t1.t[:], op=ALU.add), [S0, t1], [S0])
                    V(lambda e, b0=b0: e.tensor_tensor(out=t2.t[:].rearrange("p (b hp k) -> p b hp k", b=2, hp=4),
                                                       in0=pbs[3].t[:, :].rearrange("p (b hp k) -> p b hp k", b=2, hp=4),
                                                       in1=vT.t[:, :, b0:b0 + 2].rearrange("p hp b -> p b hp").unsqueeze(3).to_broadcast([128, 2, 4, 64]),
                                                       op=ALU.mult), [pbs[3], vT], [t2])
                    V(lambda e, Sv=Sv: e.tensor_tensor(out=Sv, in0=Sv, in1=t2.t[:], op=ALU.add), [S0, t2], [S0])
                    V(lambda e, Sv=Sv: e.tensor_tensor(out=t1.t[:], in0=Sv, in1=pbs[4].t[:, :], op=ALU.mult), [S0, pbs[4]], [t1])
                    V(lambda e, b0=b0: e.tensor_reduce(out=oS.t[:, b0:b0 + 2, :].rearrange("p b hp -> p (b hp)"),
                                                       in_=t1.t[:].rearrange("p (g k) -> p g k", k=64), axis=AX.X, op=ALU.add), [t1], [oS])
                    if bl == 6:
                        DMA(o_wkv_s[l, b0 - 6:b0 + 2].rearrange("b (hp h2) v k -> (h2 v) b hp k", h2=2), S0.t[:], reads=[S0], writes=[])
                DMA(cVap, ca_v[l].rearrange("b p n -> p b n"), reads=[], writes=[S0])
                oT = t5
                V(lambda e: e.tensor_copy(v4(oT.t[:]), oS.t[:].rearrange("p b hp -> p hp b")), [oS], [oT])
                pm = pget("m")
                MM(pm.t[:, 0:64], bo_f, oT.t[:], True, True, [cst, oT], [pm])
                V(lambda e: e.scalar_tensor_tensor(out=t6.t[:], in0=pm.t[:, 0:64], scalar=-1.0 / 64, in1=oT.t[:], op0=ALU.mult, op1=ALU.add), [pm, oT], [t6])
                V(lambda e: e.tensor_tensor(out=t7.t[:], in0=t6.t[:], in1=t6.t[:], op=ALU.mult), [t6], [t7])
                pv_ = pget("m")
                MM(pv_.t[:, 0:64], bo_f, t7.t[:], True, True, [cst, t7], [pv_])
                A(lambda e: e.activation(out=t7.t[:], in_=pv_.t[:, 0:64], func=AF.Sqrt, bias=64e-5, scale=1.0 / 64), [pv_], [t7])
                V(lambda e: e.reciprocal(t7.t[:], t7.t[:]), [t7], [t7])
                V(lambda e: e.tensor_tensor(out=t6.t[:], in0=t6.t[:], in1=t7.t[:], op=ALU.mult), [t6, t7], [t6])
                V(lambda e: e.tensor_tensor(out=v4(t6.t[:]), in0=v4(t6.t[:]), in1=colb(41), op=ALU.mult), [t6, pcol], [t6])
                V(lambda e: e.tensor_tensor(out=v4(t6.t[:]), in0=v4(t6.t[:]), in1=colb(45), op=ALU.add), [t6, pcol], [t6])
                V(lambda e: e.tensor_tensor(out=v4(t6.t[:]), in0=v4(t6.t[:]), in1=bonS.t[:], op=ALU.add), [t6, bonS], [t6])
                A(lambda e: e.activation(out=v4(t7.t[:]), in_=us.t[:, 13:17, :], func=AF.Silu), [us], [t7])
                V(lambda e: e.tensor_tensor(out=yaS.t[:], in0=v4(t6.t[:]), in1=v4(t7.t[:]), op=ALU.mult), [t6, t7], [yaS])

                a0_, a1_, a2_, a3_, a4_, a5_ = at
                f64 = lambda ap: ap.rearrange("p c b -> p (c b)")
                V(lambda e: e.tensor_tensor(out=qn64.t[:], in0=ua.t[:, 0:10, :], in1=ua.t[:, 0:10, :], op=ALU.mult), [ua], [qn64])
                pq = pget("m")
                MM(pq.t[0:64, 0:160], bo_f[0:64, 0:64], f64(qn64.t[:]), True, True, [cst, qn64], [pq])
                A(lambda e: e.activation(out=f64(qn64.t[:]), in_=pq.t[0:64, 0:160], func=AF.Sqrt, bias=1e-6, scale=1.0 / 64), [pq], [qn64])
                V(lambda e: e.reciprocal(qn64.t[:], qn64.t[:]), [qn64], [qn64])
                V(lambda e: e.tensor_tensor(out=qn64.t[:], in0=qn64.t[:], in1=ua.t[:, 0:10, :], op=ALU.mult), [qn64, ua], [qn64])
                V(lambda e: e.tensor_scalar(out=qn64.t[:, 0:8, :], in0=qn64.t[:, 0:8, :], scalar1=dcol.t[0:64, 4:5], scalar2=None, op0=ALU.mult), [qn64, dcol], [qn64])
                V(lambda e: e.tensor_scalar(out=qn64.t[:, 8:10, :], in0=qn64.t[:, 8:10, :], scalar1=pcol.t[0:64, l, 50:51], scalar2=None, op0=ALU.mult), [qn64, pcol], [qn64])
                V(lambda e: e.tensor_tensor(out=a0_.t[:].rearrange("p (g r) b -> p g r b", g=2), in0=qn64.t[:, 0:8, :].rearrange("p (g r) b -> p g r b", g=2),
                                            in1=qn64.t[:, 8:10, :].unsqueeze(2).to_broadcast([64, 2, 4, NS]), op=ALU.mult), [qn64], [a0_])
                pnw = pget("m")
                MM(pnw.t[0:64, 0:128], bo_f[0:64, 0:64], f64(a0_.t[:]), True, True, [cst, a0_], [pnw])
                b0c = smpc.t[0:64, 8 + DEPTH * 8:8 + DEPTH * 8 + 8]
                V(lambda e: e.tensor_tensor(out=a1_.t[:], in0=pnw.t[0:64, 0:128].rearrange("p (h b) -> p h b", b=NS),
                                            in1=b0c.unsqueeze(2).to_broadcast([64, 8, NS]), op=ALU.add), [pnw, smpc], [a1_])
                A(lambda e: e.activation(out=a1_.t[:], in_=a1_.t[:], func=AF.Exp), [a1_], [a1_])
                for b4 in range(NS // 4):
                    V(lambda e, b4=b4: e.tensor_tensor(
                        out=Dq.rearrange("p b (h d) -> p b h d", d=64),
                        in0=ident2[0:64, :].unsqueeze(1).unsqueeze(1).to_broadcast([64, 4, 8, 64]),
                        in1=qn64.t[:, 0:8, b4 * 4:(b4 + 1) * 4].rearrange("p h b -> p b h").unsqueeze(3).to_broadcast([64, 4, 8, 64]),
                        op=ALU.mult), [cst, qn64], [Dx])
                    for bi in range(4):
                        b = b4 * 4 + bi
                        pqb = pget("c")
                        MM(pqb.t[:, :], cst.t[0:64, K_ONE:K_ONE + 128], Dq[:, bi, :], True, True, [cst, Dx], [pqb])
                        t1 = tmpS[bi % 2]
                        V(lambda e, pqb=pqb, t1=t1, b=b: e.tensor_tensor(
                            out=t1.t[:].rearrange("p (g r d) -> p g r d", g=2, r=4),
                            in0=pqb.t[:, :].rearrange("p (g r d) -> p g r d", g=2, r=4),
                            in1=cK.t[:, b, :].rearrange("p (g d) -> p g d", g=2).unsqueeze(2).to_broadcast([128, 2, 4, 64]),
                            op=ALU.mult), [pqb, cK], [t1])
                        V(lambda e, t1=t1, b=b: e.tensor_reduce(out=scS.t[:, b, :], in_=t1.t[:].rearrange("p (h d) -> p h d", d=64), axis=AX.X, op=ALU.add), [t1], [scS])
                V(lambda e: e.tensor_tensor(out=scS.t[:], in0=scS.t[:], in1=smpc.t[:, 0:8].unsqueeze(1).to_broadcast([128, NS, 8]), op=ALU.add), [scS, smpc], [scS])
                A(lambda e: e.activation(out=pS.t[:], in_=scS.t[:], func=AF.Exp), [scS], [pS])
                pdn = banks[6]
                MM(pdn.t[0:64, 0:128], cst.t[:, K_ONE:K_ONE + 64], pS.t[:].rearrange("p b h -> p (b h)"), True, True, [cst, pS], [pdn])
                pov = banks[7]
                for b in range(NS):
                    for g in range(2):
                        MM(pov.t[0:64, b * 8 + g * 4:b * 8 + g * 4 + 4], cVap[:, b, g * 64:(g + 1) * 64], pS.t[:, b, g * 4:(g + 1) * 4], True, True, [S0, pS], [pov])
                bh = lambda ap: ap.rearrange("p (b h) -> p h b", h=8)
                es64 = smpc.t[0:64, 8 + l * 8:8 + l * 8 + 8]
                A(lambda e: e.activation(out=a5_.t[:, :, 0], in_=es64, func=AF.Exp), [smpc], [a5_])
                V(lambda e: e.tensor_tensor(out=a2_.t[:], in0=bh(pdn.t[0:64, 0:128]), in1=a1_.t[:], op=ALU.add), [pdn, a1_], [a2_])
                V(lambda e: e.tensor_tensor(out=a2_.t[:], in0=a2_.t[:], in1=a5_.t[:, :, 0:1].to_broadcast([64, 8, NS]), op=ALU.add), [a2_, a5_], [a2_])
                V(lambda e: e.reciprocal(a2_.t[:], a2_.t[:]), [a2_], [a2_])
                V(lambda e: e.tensor_tensor(out=a3_.t[:].rearrange("p (g r) b -> p g r b", g=2), in0=a1_.t[:].rearrange("p (g r) b -> p g r b", g=2),
                                            in1=ua.t[:, 10:12, :].unsqueeze(2).to_broadcast([64, 2, 4, NS]), op=ALU.mult), [a1_, ua], [a3_])
                V(lambda e: e.tensor_tensor(out=a3_.t[:], in0=a3_.t[:], in1=bh(pov.t[0:64, 0:128]), op=ALU.add), [a3_, pov], [a3_])
                V(lambda e: e.tensor_tensor(out=a3_.t[:], in0=a3_.t[:], in1=a2_.t[:], op=ALU.mult), [a3_, a2_], [a3_])
                A(lambda e: e.activation(out=a4_.t[:], in_=ua.t[:, 12:20, :], func=AF.Silu), [ua], [a4_])
                V(lambda e: e.tensor_tensor(out=yb64.t[:], in0=a3_.t[:], in1=a4_.t[:], op=ALU.mult), [a3_, a4_], [yb64])
                y4 = yb64.t[:].rearrange("p (hp h2) b -> p hp h2 b", h2=2)
                DMA(ybS.t[0:64, :, :], y4[:, :, 0, :], reads=[yb64], writes=[ybS])
                DMA(ybS.t[64:128, :, :], y4[:, :, 1, :], reads=[yb64], writes=[ybS])
                DMA(o_k_s[l][:, 0:127, :].rearrange("b p n -> p b n"), cK.t[1:128, :, :], reads=[cK], writes=[])
                DMA(o_v_s[l][:, 0:127, :].rearrange("b p n -> p b n"), cVap[1:128, :, :], reads=[S0], writes=[])
                out_res.append(cK.r)
                for (src_ap, dst_d, rr) in ((qn64.t[:, 8:10, :], o_k_s, qn64), (ua.t[:, 10:12, :], o_v_s, ua)):
                    pt = pget("m")
                    TR(pt.t[0:32, 0:64], src_ap.rearrange("p g b -> p (g b)"), identf[0:64, 0:64], [rr, cst], [pt])
                    ko = sb([32, 64])
                    A(lambda e, pt=pt, ko=ko: e.copy(ko.t[:], pt.t[0:32, 0:64]), [pt], [ko])
                    for g in range(2):
                        DMA(dst_d[l][:, 127, g * 64:(g + 1) * 64], ko.t[g * NS:(g + 1) * NS, :], reads=[ko], writes=[])
                    out_res.append(ko.r)
                pa = pget("c")
                pbb = pget("c")
                for n in range(8):
                    for c in range(4):
                        MM(pa.t[:, n * NS:(n + 1) * NS], Woa.t[:, c, n * 128:(n + 1) * 128], yaS.t[:, c, :], c == 0, c == 3, [Woa, yaS], [pa])
                    for c in range(4):
                        MM(pbb.t[:, n * NS:(n + 1) * NS], Wob.t[:, c, n * 128:(n + 1) * 128], ybS.t[:, c, :], c == 0, c == 3, [Wob, ybS], [pbb])
                gA, gB = tmpS
                A(lambda e: e.activation(out=gA.t[:, 0:128], in_=us.t[:, 27:35, :].rearrange("p c b -> p (c b)"), func=AF.Sigmoid), [us], [gA])
                A(lambda e: e.activation(out=gB.t[:, 0:128], in_=us.t[:, 35:43, :].rearrange("p c b -> p (c b)"), func=AF.Sigmoid), [us], [gB])
                V(lambda e: e.tensor_tensor(out=gA.t[:, 0:128], in0=gA.t[:, 0:128], in1=pa.t[:, 0:128], op=ALU.mult), [gA, pa], [gA])
                V(lambda e: e.tensor_tensor(out=gB.t[:, 0:128], in0=gB.t[:, 0:128], in1=pbb.t[:, 0:128], op=ALU.mult), [gB, pbb], [gB])
                V(lambda e: e.tensor_tensor(out=mS.t[:].rearrange("p c b -> p (c b)"), in0=gA.t[:, 0:128], in1=gB.t[:, 0:128], op=ALU.add), [gA, gB], [mS])
                for hf in range(2):
                    pf = pget("p")
                    for kc in range(8):
                        MM(pf.t[0:NS, :], mS.t[:, kc, :], Wout.t[:, kc, hf * 512:(hf + 1) * 512], kc == 0, kc == 7, [mS, Wout], [pf])
                    hsl = slice(hf * 512, (hf + 1) * 512)
                    V(lambda e, pf=pf, hsl=hsl: e.tensor_tensor(out=xn.t[:, hsl], in0=pf.t[0:NS, :], in1=gts.t[:, hsl], op=ALU.mult), [pf, gts], [xn])
                    V(lambda e, hsl=hsl: e.tensor_tensor(out=xs_res.t[:, hsl], in0=xs_res.t[:, hsl], in1=xn.t[:, hsl], op=ALU.add), [xs_res, xn], [xs_res])
                DMA(y_s if l == DEPTH - 1 else xs_mid, xs_res.t[:], reads=[xs_res], writes=[r_xsmid])
                if l + 1 < DEPTH:
                    load_weights(l + 1, "rest")
                S.barrier()

        try:
            _layers()
        except StopBuild as ex:
            print("STOP at", ex)
        S.barrier()
        S.emit()
    return nc


def _t5_bucket(dist):
    d = np.maximum(dist, 0)
    lr = np.log(np.maximum(d, 1).astype(np.float32) / 16) / np.float32(np.log(128 / 16))
    large = np.minimum(16 + (lr * 16).astype(np.int32), 31)
    return np.where(d < 16, d, large)


_NC_CACHE = {}


def kernel(x_prompt, x_sample, c_prompt, c_sample, state_wkv, state_shift, cache_k, cache_v,
           norm_g, w_ada, b_ada, w_in, mu_shift, w0, w_decay_up, a0, w_a_up, k_k, k_a, r_k,
           lnx_g, lnx_b, w_o_a, q_norm_g, k_norm_g, rel_bias, sinks, w_o_b, w_out):
    f = lambda a: np.ascontiguousarray(np.asarray(a, dtype=np.float32))
    x_prompt, x_sample, c_prompt, c_sample = f(x_prompt), f(x_sample), f(c_prompt), f(c_sample)
    state_wkv, state_shift, cache_k, cache_v = f(state_wkv), f(state_shift), f(cache_k), f(cache_v)
    rel_bias = f(rel_bias)
    sinks = f(sinks)
    pcol = np.zeros((128, DEPTH, NPC), np.float32)
    for l in range(DEPTH):
        pcol[:, l, 0:8] = f(norm_g)[l].reshape(8, 128).T
        pcol[:, l, 8:21] = f(mu_shift)[l].reshape(13, 128).T
        for j, a in enumerate((w0, a0, k_k, k_a, f(r_k).reshape(DEPTH, 512), lnx_g, lnx_b)):
            pcol[:, l, 21 + 4 * j:25 + 4 * j] = f(a)[l].reshape(4, 128).T
        pcol[:, l, 49] = np.tile(f(q_norm_g)[l], 2)
        pcol[:, l, 50] = np.tile(f(k_norm_g)[l], 2)
        pcol[:, l, 51:55] = np.repeat(sinks[l].reshape(4, 2), 64, axis=1).T
        pcol[:, l, 55:79] = f(b_ada)[l].reshape(24, 128).T
        pcol[:, l, 79:83] = np.repeat(rel_bias[0].reshape(4, 2), 64, axis=1).T
    bgate = np.ascontiguousarray(np.broadcast_to(f(b_ada)[:, 2048:3072][None], (18, DEPTH, D)))
    wlora = np.ascontiguousarray(np.concatenate([f(w_decay_up), f(w_a_up)], axis=1))
    tk = np.arange(128)[:, None]
    tq = np.arange(128)[None, :]
    bm = np.zeros((128, 4, 2, 2, 128), np.float32)
    for hf in range(2):
        dist = tq + 128 - tk if hf == 0 else tq - tk
        valid = (dist >= 0) & (dist <= 128)
        bk = _t5_bucket(dist)
        for h in range(8):
            bm[:, h // 2, h % 2, hf, :] = np.where(valid, rel_bias[bk, h], np.float32(NEGM))
    bm = np.ascontiguousarray(bm.reshape(128, 4, 512))
    smpc = np.zeros((128, 8 + DEPTH * 8 + 8), np.float32)
    smpc[:, 0:8] = rel_bias[_t5_bucket(128 - np.arange(128))]
    for l in range(DEPTH):
        smpc[:, 8 + l * 8:16 + l * 8] = sinks[l][None, :]
    smpc[:, 8 + DEPTH * 8:] = rel_bias[0][None, :]
    cst = np.zeros((128, K_END), np.float32)
    p = np.arange(128)[:, None]
    j = np.arange(128)[None, :]
    cst[:, K_ID:K_ID + 128] = (p == j)
    cst[:, K_MUS:K_MUS + 128] = (j > p)
    cst[:, K_MLS:K_MLS + 128] = (j < p)
    cst[:, K_MUI:K_MUI + 128] = (j >= p)
    sm = np.ones((128, 256), np.float32)
    sm[:, 0] = 0
    sm[:, 128] = 0
    cst[:, K_SCAN:K_SCAN + 256] = sm
    cst[:, K_BO:K_BO + 128] = (p // 64 == j // 64)
    cst[:, K_ID2:K_ID2 + 64] = (p % 64 == np.arange(64)[None, :])
    selm = np.zeros((128, 256), np.float32)
    selm[16, 0:128] = 1
    selm[17, 128:256] = 1
    cst[:, K_SEL:K_SEL + 256] = selm
    cst[:, K_ONE:K_ONE + 128] = 1.0
    return _run(locals())


def _run(v):
    f = lambda a: np.ascontiguousarray(np.asarray(a, dtype=np.float32))
    if "nc" not in _NC_CACHE:
        _NC_CACHE["nc"] = build_nc()
    nc = _NC_CACHE["nc"]
    shared = dict(w_ada=f(v["w_ada"]), w_in=f(v["w_in"]), w_o_a=f(v["w_o_a"]), w_o_b=f(v["w_o_b"]), w_out=f(v["w_out"]),
                  wlora=v["wlora"], pcol=v["pcol"], bgate=v["bgate"], bm=v["bm"], cst=v["cst"], smpc=v["smpc"])
    in_maps = []
    for c in range(NCORES):
        sl = slice(c * NS, (c + 1) * NS)
        m = dict(shared)
        m["xp"] = np.ascontiguousarray(v["x_prompt"][c * NP:(c + 1) * NP])
        m["xs"] = np.ascontiguousarray(v["x_sample"][sl, 0, :])
        m["c_all"] = np.ascontiguousarray(np.concatenate([v["c_sample"][sl], v["c_prompt"][c * NP:(c + 1) * NP]], axis=0))
        m["st_wkv"] = np.ascontiguousarray(v["state_wkv"][:, sl])
        m["st_shift"] = np.ascontiguousarray(v["state_shift"][:, sl])
        m["ca_k"] = np.ascontiguousarray(v["cache_k"][:, sl].reshape(DEPTH, NS, 128, 128))
        m["ca_v"] = np.ascontiguousarray(v["cache_v"][:, sl].reshape(DEPTH, NS, 128, 128))
        in_maps.append(m)
    res = run_bass_kernel_spmd(nc, in_maps, core_ids=list(range(NCORES)))
    R = res.results
    cat = lambda k, ax: np.concatenate([np.asarray(r[k], dtype=np.float32) for r in R], axis=ax)
    y_prompt = cat("y_p", 0)
    y_sample = cat("y_s", 0).reshape(128, 1, D)
    wkv_p = cat("o_wkv_p", 1)
    sh_p = cat("o_sh_p", 1)
    k_p = cat("o_k_p", 1).reshape(DEPTH, 16, 128, 2, 64)
    v_p = cat("o_v_p", 1).reshape(DEPTH, 16, 128, 2, 64)
    wkv_s = cat("o_wkv_s", 1)
    sh_s = cat("o_sh_s", 1)
    k_s = cat("o_k_s", 1).reshape(DEPTH, 128, 128, 2, 64)
    v_s = cat("o_v_s", 1).reshape(DEPTH, 128, 128, 2, 64)
    return (y_prompt, y_sample, wkv_p, sh_p, k_p, v_p, wkv_s, sh_s, k_s, v_s)
```

```python
import numpy as np
import concourse.bass as bass
import concourse.mybir as mybir
from concourse.bass_utils import run_bass_kernel_spmd
from contextlib import ExitStack

F32 = mybir.dt.float32
BF16 = mybir.dt.bfloat16
AF = mybir.ActivationFunctionType
ALU = mybir.AluOpType
AX = mybir.AxisListType

NCORES = 8
D = 1024
SEQ = 2048
DEPTH = 2
NS = 16
NP = 2
TB = 256
NBLK = SEQ // TB
SHIFT_W = 1664
INC = 5504
WINC = INC + 256
C_R, C_K, C_V, C_WA, C_ZA, C_Q, C_KB, C_VB, C_ZB, C_GA, C_GB = 0, 512, 1024, 1536, 1664, 2176, 2688, 2816, 2944, 3456, 4480
C_KD = INC
NPC = 83
DC = 0.6065306597126334
NEGM = -30000.0
K_ID, K_MUS, K_MLS, K_MUI, K_SCAN, K_BO, K_ID2, K_SEL, K_ONE, K_END = 0, 128, 256, 384, 512, 768, 896, 960, 1216, 1344


class Res:
    __slots__ = ("w", "r")

    def __init__(self):
        self.w = {}
        self.r = {}


class Sched:
    ENG = ("tensor", "vector", "scalar", "gpsimd", "sync")

    def __init__(self, nc, ctx, ndma=8):
        self.nc = nc
        self.th = {e: [] for e in self.ENG}
        self.sem = {}
        self.cnt = {}
        self.known = {e: {} for e in self.ENG}
        for e in ("tensor", "vector", "scalar", "gpsimd"):
            self.sem[e] = ctx.enter_context(nc.semaphore("s_" + e))
            self.cnt[e] = 0
        self.dsem = {}
        self.dcnt = {}
        self.drr = {}
        for q in ("sync", "gpsimd"):
            self.dsem[q] = [ctx.enter_context(nc.semaphore(f"d_{q}{i}")) for i in range(ndma)]
            self.dcnt[q] = [0] * ndma
            self.drr[q] = 0
        self.same_sync = {"tensor": False, "vector": True, "scalar": True, "gpsimd": True, "sync": False}
        self.ctx = ctx
        self.last_tok = {}
        self.epoch = {e: 0 for e in self.sem}

    def _deps(self, e, reads, writes):
        need = {}
        for r in reads:
            for s, v in r.w.items():
                if need.get(s, 0) < v:
                    need[s] = v
        for w in writes:
            for d in (w.w, w.r):
                for s, v in d.items():
                    if need.get(s, 0) < v:
                        need[s] = v
        kn = self.known[e]
        out = []
        own = self.sem.get(e)
        for s, v in need.items():
            if s is own and not self.same_sync[e]:
                continue
            if kn.get(s, 0) < v:
                kn[s] = v
                out.append((s, v))
        return out

    def _commit(self, tok, reads, writes):
        s, v = tok
        for r in reads:
            r.r[s] = v
        for w in writes:
            if w.r:
                w.w = {s: v}
                w.r = {}
            else:
                w.w[s] = v

    def op(self, e, fn, reads=(), writes=()):
        _tick(e)
        waits = self._deps(e, reads, writes)
        if self.cnt[e] >= 20000:
            self.epoch[e] += 1
            self.sem[e] = self.ctx.enter_context(self.nc.semaphore(f"s_{e}_{self.epoch[e]}"))
            self.cnt[e] = 0
        self.cnt[e] += 1
        sem = self.sem[e]

        eng = getattr(self.nc, e)
        for s, v in waits:
            eng.wait_ge(s, v)
        fn(eng).then_inc(sem, 1)
        self.last_tok[e] = (sem, self.cnt[e])
        self._commit((sem, self.cnt[e]), reads, writes)

    def dma(self, q, out, in_, reads=(), writes=()):
        i = self.drr[q]
        self.drr[q] = (i + 1) % len(self.dsem[q])
        sem = self.dsem[q][i]
        prev = self.dcnt[q][i]
        waits = self._deps(q, reads, writes)
        kn = self.known[q]
        if prev and kn.get(sem, 0) < prev:
            kn[sem] = prev
            waits.append((sem, prev))
        self.dcnt[q][i] = prev + 16

        eng = getattr(self.nc, q)
        for s, v in waits:
            eng.wait_ge(s, v)
        eng.dma_start(out=out, in_=in_).then_inc(sem, 16)
        self._commit((sem, prev + 16), reads, writes)

    def barrier(self):
        toks = {}
        for e, c in self.cnt.items():
            if c:
                toks[self.sem[e]] = c
        for q in self.dsem:
            for i, s in enumerate(self.dsem[q]):
                if self.dcnt[q][i]:
                    toks[s] = self.dcnt[q][i]
        for e in self.ENG:
            kn = self.known[e]
            own = self.sem.get(e)
            waits = []
            for s, v in toks.items():
                if s is own:
                    continue
                if kn.get(s, 0) < v:
                    kn[s] = v
                    waits.append((s, v))

            eng = getattr(self.nc, e)
            for s, v in waits:
                eng.wait_ge(s, v)

    def emit(self):
        return
        with self.nc.Block() as block:
            for e in self.ENG:
                ths = self.th[e]
                if not ths:
                    continue

                def body(eng, ths=ths):
                    for t in ths:
                        t(eng)

                getattr(block, e)(body)


class StopBuild(Exception):
    pass


import os
MK_STOP = os.environ.get("MK_STOP", "")


MK_STOPN = int(os.environ.get("MK_STOPN", "0"))
_CNT = {"on": False, "n": 0}


def stop_at(tag):
    if MK_STOP == tag:
        raise StopBuild(tag)
    if MK_STOPN and tag == "xload":
        _CNT["on"] = True


def _tick(desc):
    if _CNT["on"]:
        _CNT["n"] += 1
        if _CNT["n"] == MK_STOPN:
            import traceback
            fr = traceback.extract_stack()
            print("STOPN last op:", desc, [f"{f.lineno}" for f in fr[-6:-1]])
        if _CNT["n"] > MK_STOPN:
            _CNT["on"] = False
            raise StopBuild(f"n={MK_STOPN}")


class T:
    __slots__ = ("t", "r")

    def __init__(self, t):
        self.t = t
        self.r = Res()

    def __getitem__(self, k):
        return self.t[k]


def build_nc():
    nc = bass.Bass("TRN2", target_bir_lowering=False)
    dt_in = lambda n, s: nc.dram_tensor(n, list(s), F32, kind="ExternalInput").ap()
    dt_out = lambda n, s: nc.dram_tensor(n, list(s), F32, kind="ExternalOutput").ap()
    xp = dt_in("xp", [NP, SEQ, D])
    xs_in = dt_in("xs", [NS, D])
    c_all = dt_in("c_all", [18, D])
    st_wkv = dt_in("st_wkv", [DEPTH, NS, 8, 64, 64])
    st_shift = dt_in("st_shift", [DEPTH, NS, SHIFT_W])
    ca_k = dt_in("ca_k", [DEPTH, NS, 128, 128])
    ca_v = dt_in("ca_v", [DEPTH, NS, 128, 128])
    w_ada = dt_in("w_ada", [DEPTH, D, 3 * D])
    w_in = dt_in("w_in", [DEPTH, D, INC])
    w_o_a = dt_in("w_o_a", [DEPTH, 512, D])
    w_o_b = dt_in("w_o_b", [DEPTH, 512, D])
    w_out = dt_in("w_out", [DEPTH, D, D])
    wlora = dt_in("wlora", [DEPTH, 128, 512])
    pcol_d = dt_in("pcol", [128, DEPTH, NPC])
    bgate_d = dt_in("bgate", [18, DEPTH, D])
    bm_d = dt_in("bm", [128, 4, 512])
    cst_d = dt_in("cst", [128, K_END])
    smp_d = dt_in("smpc", [128, 8 + DEPTH * 8 + 8])

    y_p = dt_out("y_p", [NP, SEQ, D])
    y_s = dt_out("y_s", [NS, D])
    o_wkv_p = dt_out("o_wkv_p", [DEPTH, NP, 8, 64, 64])
    o_sh_p = dt_out("o_sh_p", [DEPTH, NP, SHIFT_W])
    o_k_p = dt_out("o_k_p", [DEPTH, NP, 128, 128])
    o_v_p = dt_out("o_v_p", [DEPTH, NP, 128, 128])
    o_wkv_s = dt_out("o_wkv_s", [DEPTH, NS, 8, 64, 64])
    o_sh_s = dt_out("o_sh_s", [DEPTH, NS, SHIFT_W])
    o_k_s = dt_out("o_k_s", [DEPTH, NS, 128, 128])
    o_v_s = dt_out("o_v_s", [DEPTH, NS, 128, 128])
    xmid = nc.dram_tensor("xmid", [NP, SEQ, D], F32).ap()
    xs_mid = nc.dram_tensor("xs_mid", [NS, D], F32).ap()
    gate_d = nc.dram_tensor("gate_d", [DEPTH, 18, D], F32).ap()
    r_gd = [Res() for _ in range(DEPTH)]
    r_xsmid = Res()
    r_xmid = [[Res() for _ in range(NBLK)] for _ in range(NP)]
    out_res = []

    with ExitStack() as ctx:
        S = Sched(nc, ctx)
        off = [16512]
        uid = [0]

        def sb(shape, dt=F32):
            nb = int(np.prod(shape[1:])) * (4 if dt == F32 else 2)
            o = off[0]
            off[0] += (nb + 31) // 32 * 32
            assert off[0] <= 229344, f"SBUF overflow {off[0]}"
            uid[0] += 1
            return T(nc.alloc_sbuf_tensor_at(f"t{uid[0]}", list(shape), dt, offset=o))

        banks = []
        for i in range(8):
            b = T(nc.alloc_psum_tensor(f"pb{i}", [128, 512], F32))
            banks.append(b)
        prr = {"p": [0, [0, 1]], "c": [0, [2, 3, 4]], "m": [0, [5, 7]]}

        def pget(g):
            st = prr[g]
            b = banks[st[1][st[0] % len(st[1])]]
            st[0] += 1
            return b

        def V(fn, reads=(), writes=()):
            S.op("vector", fn, [x.r for x in reads], [x.r for x in writes])

        def A(fn, reads=(), writes=()):
            S.op("scalar", fn, [x.r for x in reads], [x.r for x in writes])

        def G(fn, reads=(), writes=()):
            S.op("gpsimd", fn, [x.r for x in reads], [x.r for x in writes])

        last_rg = [(0, 128)]

        def pe_rg(lhsT):
            rg = (lhsT.base_partition(), lhsT.partition_size())
            if rg != last_rg[0] and S.last_tok.get("tensor"):
                sem, val = S.last_tok["tensor"]
                if S.known["tensor"].get(sem, 0) < val:
                    S.known["tensor"][sem] = val
                    nc.tensor.wait_ge(sem, val)
            last_rg[0] = rg

        def MM(out, lhsT, rhs, start, stop, reads, writes):
            pe_rg(lhsT)
            S.op("tensor", lambda e: e.matmul(out, lhsT, rhs, start=start, stop=stop),
                 [x.r for x in reads], [x.r for x in writes])

        def TR(out, in_, ident, reads, writes):
            pe_rg(in_)
            S.op("tensor", lambda e: e.transpose(out, in_, ident), [x.r for x in reads], [x.r for x in writes])

        def DMA(out, in_, reads=(), writes=(), q="sync"):
            S.dma(q, out, in_, [x if isinstance(x, Res) else x.r for x in reads],
                  [x if isinstance(x, Res) else x.r for x in writes])

        Win = sb([128, 8, WINC], BF16)
        Woa = sb([128, 4, D], BF16)
        Wob = sb([128, 4, D], BF16)
        Wout = sb([128, 8, D], BF16)
        Wlo = sb([128, 512], BF16)
        cst = sb([128, K_END])
        pcol = sb([128, DEPTH, NPC])
        dcol = sb([128, 16])
        smpc = sb([128, 8 + DEPTH * 8 + 8])
        identb = sb([128, 128], BF16)
        bo_b = sb([128, 128], BF16)
        bo64_b = sb([128, 128], BF16)
        ones_b = sb([128, 64], BF16)
        modT = sb([128, 16, 18])
        AT = sb([128, 8, 18])
        gate_bc = sb([128, D])
        scT = sb([128, 8, 18])
        ssq = sb([128, 8])
        base_off = off[0]

        identf = cst.t[:, K_ID:K_ID + 128]
        m_us = cst.t[:, K_MUS:K_MUS + 128]
        m_ls = cst.t[:, K_MLS:K_MLS + 128]
        m_ui = cst.t[:, K_MUI:K_MUI + 128]
        scanm = cst.t[:, K_SCAN:K_SCAN + 256]
        bo_f = cst.t[:, K_BO:K_BO + 128]
        ident2 = cst.t[:, K_ID2:K_ID2 + 64]
        sel = cst.t[0:18, K_SEL:K_SEL + 256]

        DMA(cst.t[:], cst_d, writes=[cst])
        DMA(pcol.t[:], pcol_d, writes=[pcol])
        DMA(smpc.t[:], smp_d, writes=[smpc])
        V(lambda e: e.tensor_copy(identb.t[:], identf), [cst], [identb])
        V(lambda e: e.tensor_copy(bo_b.t[:], bo_f), [cst], [bo_b])
        V(lambda e: e.tensor_scalar(out=bo64_b.t[:], in0=bo_f, scalar1=1.0 / 64, scalar2=None, op0=ALU.mult), [cst], [bo64_b])
        V(lambda e: e.memset(ones_b.t[:], 1.0), [], [ones_b])

        def pc(l, j, n=1):
            return pcol.t[:, l, j:j + n]

        def _layers():
            for l in range(DEPTH):
                off[0] = base_off
                def load_weights(lw, part):
                    wv = w_in[lw].rearrange("(kc p) n -> p kc n", p=128)
                    if part == "win":
                        for kc in range(8):
                            for q4 in range(4):
                                c0 = q4 * 1376
                                DMA(Win.t[:, kc, c0:c0 + 1376], wv[:, kc, c0:c0 + 1376], writes=[Win], q="gpsimd")
                        for g in range(2):
                            for cp in range(2):
                                DMA(Win.t[:, :, C_KD + g * 128 + cp * 64:C_KD + g * 128 + cp * 64 + 64],
                                    wv[:, :, C_KB + g * 64:C_KB + g * 64 + 64], writes=[Win], q="gpsimd")
                    else:
                        DMA(Woa.t[:], w_o_a[lw].rearrange("(c p) n -> p c n", p=128), writes=[Woa], q="gpsimd")
                        DMA(Wob.t[:], w_o_b[lw].rearrange("(c p) n -> p c n", p=128), writes=[Wob], q="gpsimd")
                        DMA(Wout.t[:], w_out[lw].rearrange("(c p) n -> p c n", p=128), writes=[Wout], q="gpsimd")
                        DMA(Wlo.t[:], wlora[lw], writes=[Wlo], q="gpsimd")

                if l == 0:
                    load_weights(0, "win")
                    load_weights(0, "rest")
                V(lambda e: e.tensor_scalar(out=dcol.t[:, 0:4], in0=pc(l, 33, 4), scalar1=-1.0, scalar2=1.0, op0=ALU.mult, op1=ALU.add), [pcol], [dcol])
                V(lambda e: e.tensor_scalar(out=dcol.t[:, 4:5], in0=pc(l, 49), scalar1=0.125, scalar2=None, op0=ALU.mult), [pcol], [dcol])
                A(lambda e: e.activation(out=dcol.t[:, 5:9], in_=pc(l, 51, 4), func=AF.Exp), [pcol], [dcol])

                call = sb([18, D])
                gate_tok = sb([18, D])
                bg = sb([18, D])
                stage = [sb([128, 8, 512]), sb([128, 8, 512])]
                DMA(call.t[:], c_all, writes=[call])
                DMA(bg.t[:], bgate_d[:, l, :], writes=[bg])
                for kc in range(8):
                    pb = pget("m")
                    TR(pb.t[:, 0:18], call.t[0:18, kc * 128:(kc + 1) * 128], identf[0:18, 0:18], [call, cst], [pb])
                    A(lambda e, pb=pb, kc=kc: e.activation(out=scT.t[:, kc, :], in_=pb.t[:, 0:18], func=AF.Silu), [pb], [scT])
                wav = w_ada[l].rearrange("(kc p) n -> p kc n", p=128)
                for pcs in range(6):
                    stg = stage[pcs % 2]
                    for kh in range(2):
                        DMA(stg.t[:, kh * 4:(kh + 1) * 4, :], wav[:, kh * 4:(kh + 1) * 4, pcs * 512:(pcs + 1) * 512], writes=[stg])
                    if pcs < 4:
                        for j in range(4):
                            ncnk = pcs * 4 + j
                            pb = pget("m")
                            for kc in range(8):
                                MM(pb.t[:, 0:18], stg.t[:, kc, j * 128:(j + 1) * 128], scT.t[:, kc, :], kc == 0, kc == 7, [stg, scT], [pb])
                            A(lambda e, pb=pb, ncnk=ncnk: e.activation(out=modT.t[:, ncnk, :], in_=pb.t[:, 0:18], func=AF.Identity,
                                                                       bias=pc(l, 55 + ncnk), scale=1.0), [pb, pcol], [modT])
                    else:
                        hf = pcs - 4
                        pb = pget("m")
                        for kc in range(8):
                            MM(pb.t[0:18, :], scT.t[:, kc, :], stg.t[:, kc, :], kc == 0, kc == 7, [stg, scT], [pb])
                        V(lambda e, pb=pb, hf=hf: e.tensor_tensor(out=gate_tok.t[:, hf * 512:(hf + 1) * 512], in0=pb.t[0:18, :],
                                                                  in1=bg.t[:, hf * 512:(hf + 1) * 512], op=ALU.add), [pb, bg], [gate_tok])
                DMA(gate_d[l], gate_tok.t[:], reads=[gate_tok], writes=[r_gd[l]])
                for kc in range(8):
                    V(lambda e, kc=kc: e.tensor_scalar(out=AT.t[:, kc, :], in0=modT.t[:, 8 + kc, :], scalar1=1.0, scalar2=pc(l, kc),
                                                       op0=ALU.add, op1=ALU.mult), [modT, pcol], [AT])
                S.barrier()
                off[0] = base_off
                stop_at(f"mod{l}")

                hT = sb([128, 8, TB], BF16)
                xts = [sb([128, D]), sb([128, D])]
                yaT = sb([128, 4, TB], BF16)
                ybT = sb([128, 4, TB], BF16)
                mT = sb([128, 8, TB], BF16)
                qnT = sb([128, 4, TB], BF16)
                knT = sb([128, 2, 128 + TB], BF16)
                vbtok = sb([128, 3, 128], BF16)
                opA = sb([128, TB], BF16)
                opB = sb([128, TB], BF16)
                opK = sb([128, TB], BF16)
                opR = sb([128, TB], BF16)
                vtok = sb([128, 2, 128], BF16)
                btok = sb([128, 2, 128], BF16)
                ktok = sb([128, 2, 128], BF16)
                Hst = [sb([128, 64]) for _ in range(4)]
                Hbf = [sb([128, 128], BF16), sb([128, 128], BF16)]
                for hb_ in Hbf:
                    V(lambda e, hb_=hb_: e.memset(hb_.t[:], 0.0), [], [hb_])
                carry = sb([128, 13])
                Rraw = [sb([128, TB + 1]), sb([128, TB + 1])]
                BMt = sb([128, 512])
                Qm = sb([128, 2, 128])
                Nm = sb([128, 2, 128])
                Arb = sb([128, 2, 128], BF16)
                Aak = sb([128, 2, 128], BF16)
                Ark = sb([128, 2, 128], BF16)
                Qp = [sb([128, 2, 128]), sb([128, 2, 128])]
                Np_ = [sb([128, 2, 128]), sb([128, 2, 128])]
                Yb = [sb([128, 2, 128]), sb([128, 2, 128])]
                Ybf = sb([128, 2, 128], BF16)
                CS = [dict(Qm=Qm, Nm=Nm, Arb=Arb, Aak=Aak, Ark=Ark, Qp=Qp, Np=Np_, Yb=Yb, Ybf=Ybf),
                      dict(Qm=sb([128, 2, 128]), Nm=sb([128, 2, 128]), Arb=sb([128, 2, 128], BF16), Aak=sb([128, 2, 128], BF16),
                           Ark=sb([128, 2, 128], BF16), Qp=[sb([128, 2, 128]), sb([128, 2, 128])], Np=[sb([128, 2, 128]), sb([128, 2, 128])],
                           Yb=[sb([128, 2, 128]), sb([128, 2, 128])], Ybf=sb([128, 2, 128], BF16))]
                if os.environ.get("MK_VERBOSE"):
                    print("prompt-phase SBUF end", off[0], "limit 229344")
                W0sb = sb([128, 128], BF16)
                Usb = sb([128, 128], BF16)
                Fp = [sb([128, TB]) for _ in range(8)]
                Bp = [sb([128, TB], BF16) for _ in range(4)]
                dXR, dXK, dXV, dEP, dBON, dKM, dAG = [sb([128, TB]) for _ in range(7)]
                dTWD = sb([128, TB], BF16)
                frr = [0]
                brr = [0]

                def fget():
                    t = Fp[frr[0] % len(Fp)]
                    frr[0] += 1
                    return t

                def bget():
                    t = Bp[brr[0] % len(Bp)]
                    brr[0] += 1
                    return t


                def proj(col0, ncols=128):
                    pb = pget("p")
                    for kc in range(8):
                        MM(pb.t[0:ncols, 0:TB], Win.t[:, kc, col0:col0 + ncols], hT.t[:, kc, :], kc == 0, kc == 7, [Win, hT], [pb])
                    return pb

                xpf = [False]
                for s in range(NP):
                    DMA(gate_bc.t[:], gate_d[l, 16 + s:17 + s, :].partition_broadcast(128), reads=[r_gd[l]], writes=[gate_bc])
                    for hp in range(4):
                        V(lambda e, hp=hp: e.memset(Hst[hp].t[:], 0.0), [], [Hst[hp]])
                    V(lambda e: e.memset(carry.t[:], 0.0), [], [carry])
                    V(lambda e: e.memset(knT.t[:, :, 0:128], 0.0), [], [knT])
                    V(lambda e: e.memset(vbtok.t[:, 0, :], 0.0), [], [vbtok])
                    for blk in range(NBLK):
                        t0 = blk * TB
                        xsrc = xp if l == 0 else xmid
                        xdst = xmid if l == 0 else y_p
                        rd_x = [r_xmid[s][blk]] if l == 1 else []
                        last = blk == NBLK - 1
                        for ti in range(TB // 128):
                            xt = xts[ti % 2]
                            if ti == 0 and xpf[0]:
                                xpf[0] = False
                            else:
                                DMA(xt.t[:], xsrc[s, t0 + ti * 128:t0 + (ti + 1) * 128, :], reads=rd_x, writes=[xt])
                            jk = fget()
                            V(lambda e: e.memset(ssq.t[:, 4:8], 0.0), [], [ssq])
                            for hh in range(4):
                                A(lambda e, hh=hh, jk=jk: e.activation(out=jk.t[:, :], in_=xt.t[:, hh * 256:(hh + 1) * 256], func=AF.Square,
                                                                       accum_out=ssq.t[:, 4 + hh:5 + hh]), [xt, ssq], [jk, ssq])
                            V(lambda e: e.tensor_reduce(out=ssq.t[:, 0:1], in_=ssq.t[:, 4:8], axis=AX.X, op=ALU.add), [ssq], [ssq])
                            A(lambda e: e.activation(out=ssq.t[:, 1:2], in_=ssq.t[:, 0:1], func=AF.Ln, bias=1e-6, scale=1.0 / D), [ssq], [ssq])
                            A(lambda e: e.activation(out=ssq.t[:, 2:3], in_=ssq.t[:, 1:2], func=AF.Exp, scale=-0.5), [ssq], [ssq])
                            V(lambda e: e.tensor_scalar(out=xt.t[:], in0=xt.t[:], scalar1=ssq.t[:, 2:3], scalar2=None, op0=ALU.mult), [xt, ssq], [xt])
                            for kh in range(2):
                                pb = pget("m")
                                for k4 in range(4):
                                    kc = kh * 4 + k4
                                    TR(pb.t[:, k4 * 128:(k4 + 1) * 128], xt.t[:, kc * 128:(kc + 1) * 128], identf, [xt, cst], [pb])
                                for k4 in range(4):
                                    kc = kh * 4 + k4
                                    A(lambda e, pb=pb, k4=k4, kc=kc, ti=ti: e.activation(
                                        out=hT.t[:, kc, ti * 128:(ti + 1) * 128], in_=pb.t[:, k4 * 128:(k4 + 1) * 128], func=AF.Identity,
                                        bias=modT.t[:, kc, 16 + s:17 + s], scale=AT.t[:, kc, 16 + s:17 + s]), [pb, modT, AT], [hT])

                        def shifted(cidx, col0, xo):
                            pb = proj(col0)
                            R = Rraw[cidx % 2]
                            A(lambda e: e.copy(R.t[:, 1:TB + 1], pb.t[:, 0:TB]), [pb], [R])
                            G(lambda e: e.tensor_copy(R.t[:, 0:1], carry.t[:, cidx:cidx + 1]), [carry], [R])
                            G(lambda e: e.tensor_copy(carry.t[:, cidx:cidx + 1], R.t[:, TB:TB + 1]), [R], [carry])
                            dd = fget()
                            G(lambda e: e.tensor_tensor(out=dd.t[:], in0=R.t[:, 0:TB], in1=R.t[:, 1:TB + 1], op=ALU.subtract), [R], [dd])
                            V(lambda e: e.scalar_tensor_tensor(out=xo.t[:], in0=dd.t[:], scalar=pc(l, 8 + cidx), in1=R.t[:, 1:TB + 1],
                                                               op0=ALU.mult, op1=ALU.add), [dd, R, pcol], [xo])
                            return xo

                        stop_at('xload')
                        wa = shifted(12, C_WA, fget())
                        twd = dTWD
                        A(lambda e: e.activation(out=twd.t[0:64, :], in_=wa.t[0:64, :], func=AF.Tanh), [wa], [twd])
                        V(lambda e: e.tensor_copy(twd.t[64:128, :], wa.t[64:128, :]), [wa], [twd])
                        for hp in range(4):
                            xr = shifted(hp, C_R + hp * 128, dXR)
                            xk = shifted(4 + hp, C_K + hp * 128, dXK)
                            xv = shifted(8 + hp, C_V + hp * 128, dXV)
                            pw = pget("m")
                            MM(pw.t[:, 0:TB], Wlo.t[0:64, hp * 128:(hp + 1) * 128], twd.t[0:64, :], True, True, [Wlo, twd], [pw])
                            MM(pw.t[:, TB:2 * TB], Wlo.t[64:128, hp * 128:(hp + 1) * 128], twd.t[64:128, :], True, True, [Wlo, twd], [pw])
                            sg = fget()
                            ag = dAG
                            A(lambda e: e.activation(out=sg.t[:], in_=pw.t[:, 0:TB], func=AF.Sigmoid, bias=pc(l, 21 + hp), scale=1.0), [pw, pcol], [sg])
                            A(lambda e: e.activation(out=ag.t[:], in_=pw.t[:, TB:2 * TB], func=AF.Sigmoid, bias=pc(l, 25 + hp), scale=1.0), [pw, pcol], [ag])
                            Gc = fget()
                            V(lambda e: e.tensor_tensor_scan(out=Gc.t[:], data0=scanm, data1=sg.t[:], initial=0.0, op0=ALU.mult, op1=ALU.add), [cst, sg], [Gc])
                            Gx = fget()
                            G(lambda e: e.tensor_tensor(out=Gx.t[:], in0=Gc.t[:], in1=sg.t[:], op=ALU.subtract), [Gc, sg], [Gx])
                            Ep = dEP
                            Em = fget()
                            Ex = fget()
                            A(lambda e: e.activation(out=Ep.t[:], in_=Gc.t[:], func=AF.Exp, scale=-DC), [Gc], [Ep])
                            A(lambda e: e.activation(out=Em.t[:], in_=Gc.t[:], func=AF.Exp, scale=DC), [Gc], [Em])
                            A(lambda e: e.activation(out=Ex.t[:], in_=Gx.t[:], func=AF.Exp, scale=-DC), [Gx], [Ex])
                            kr = fget()
                            V(lambda e: e.tensor_scalar(out=kr.t[:], in0=xk.t[:], scalar1=pc(l, 29 + hp), scalar2=None, op0=ALU.mult), [xk, pcol], [kr])
                            sq = bget()
                            A(lambda e: e.activation(out=sq.t[:], in_=kr.t[:], func=AF.Square), [kr], [sq])
                            pn = pget("m")
                            MM(pn.t[:, 0:TB], bo_b.t[:], sq.t[:], True, True, [bo_b, sq], [pn])
                            nr = fget()
                            A(lambda e: e.activation(out=nr.t[:], in_=pn.t[:, 0:TB], func=AF.Ln, bias=1e-24, scale=1.0), [pn], [nr])
                            A(lambda e: e.activation(out=nr.t[:], in_=nr.t[:], func=AF.Exp, scale=-0.5), [nr], [nr])
                            kk = fget()
                            V(lambda e: e.tensor_tensor(out=kk.t[:], in0=kr.t[:], in1=nr.t[:], op=ALU.mult), [kr, nr], [kk])
                            km = dKM
                            V(lambda e: e.tensor_scalar(out=km.t[:], in0=ag.t[:], scalar1=pc(l, 33 + hp), scalar2=dcol.t[:, hp:hp + 1],
                                                        op0=ALU.mult, op1=ALU.add), [ag, pcol, dcol], [km])
                            G(lambda e: e.tensor_tensor(out=km.t[:], in0=km.t[:], in1=xk.t[:], op=ALU.mult), [km, xk], [km])
                            V(lambda e: e.scalar_tensor_tensor(out=opA.t[:], in0=kk.t[:], scalar=-1.0, in1=Ex.t[:], op0=ALU.mult, op1=ALU.mult), [kk, Ex], [opA])
                            kka = fget()
                            G(lambda e: e.tensor_tensor(out=kka.t[:], in0=kk.t[:], in1=ag.t[:], op=ALU.mult), [kk, ag], [kka])
                            V(lambda e: e.tensor_tensor(out=opB.t[:], in0=kka.t[:], in1=Em.t[:], op=ALU.mult), [kka, Em], [opB])
                            V(lambda e: e.tensor_tensor(out=opK.t[:], in0=km.t[:], in1=Em.t[:], op=ALU.mult), [km, Em], [opK])
                            V(lambda e: e.tensor_tensor(out=opR.t[:], in0=xr.t[:], in1=Ep.t[:], op=ALU.mult), [xr, Ep], [opR])
                            rk = bget()
                            V(lambda e: e.scalar_tensor_tensor(out=rk.t[:], in0=xr.t[:], scalar=pc(l, 37 + hp), in1=km.t[:], op0=ALU.mult, op1=ALU.mult), [xr, km, pcol], [rk])
                            pbn = pget("m")
                            MM(pbn.t[:, 0:TB], bo_b.t[:], rk.t[:], True, True, [bo_b, rk], [pbn])
                            bon = dBON
                            V(lambda e: e.tensor_tensor(out=bon.t[:], in0=pbn.t[:, 0:TB], in1=xv.t[:], op=ALU.mult), [pbn, xv], [bon])
                            vb16 = bget()
                            A(lambda e: e.copy(vb16.t[:], xv.t[:]), [xv], [vb16])
                            for (src, dst) in ((vb16, vtok), (opB, btok), (opK, ktok)):
                                pt = pget("m")
                                ptb = pt.t[:].bitcast(BF16)
                                for c in range(2):
                                    TR(ptb[:, c * 128:(c + 1) * 128], src.t[:, c * 128:(c + 1) * 128], identb.t[:], [src, identb], [pt])
                                A(lambda e, ptb=ptb, dst=dst: e.copy(dst.t[:].rearrange("p c t -> p (c t)"), ptb[:, 0:256]), [pt], [dst])
                            stop_at('r1')
                            H = Hst[hp]
                            pO = banks[6]
                            for c in range(2):
                                cs = slice(c * 128, (c + 1) * 128)
                                B = CS[c]
                                Qm, Nm, Arb, Aak, Ark, Yb = B["Qm"], B["Nm"], B["Arb"], B["Aak"], B["Ark"], B["Yb"]
                                pA_ = pget("c")
                                pB_ = pget("c")
                                pC_ = pget("c")
                                for h2 in ((0, 1) if c == 0 else (1, 0)):
                                    ps_ = slice(h2 * 64, (h2 + 1) * 64)
                                    hs = slice(h2 * 128, (h2 + 1) * 128)
                                    MM(pA_.t[:, hs], opB.t[ps_, cs], opA.t[ps_, cs], True, True, [opA, opB], [pA_])
                                    MM(pA_.t[:, 256 + h2 * 128:256 + (h2 + 1) * 128], opA.t[ps_, cs], opB.t[ps_, cs], True, True, [opA, opB], [pA_])
                                    MM(pB_.t[:, hs], opB.t[ps_, cs], opR.t[ps_, cs], True, True, [opB, opR], [pB_])
                                    MM(pB_.t[:, 256 + h2 * 128:256 + (h2 + 1) * 128], opK.t[ps_, cs], opA.t[ps_, cs], True, True, [opK, opA], [pB_])
                                    MM(pC_.t[:, h2 * 128:(h2 + 1) * 128], opK.t[ps_, cs], opR.t[ps_, cs], True, True, [opK, opR], [pC_])
                                bc2 = lambda m: m.unsqueeze(1).to_broadcast([128, 2, 128])
                                v3 = lambda ap: ap.rearrange("p (h t) -> p h t", h=2)
                                V(lambda e: e.tensor_tensor(out=Qm.t[:], in0=v3(pA_.t[:, 0:256]), in1=bc2(m_us), op=ALU.mult), [pA_, cst], [Qm])
                                V(lambda e: e.tensor_tensor(out=Nm.t[:], in0=v3(pA_.t[:, 256:512]), in1=bc2(m_ls), op=ALU.mult), [pA_, cst], [Nm])
                                V(lambda e: e.tensor_tensor(out=Arb.t[:], in0=v3(pB_.t[:, 0:256]), in1=bc2(m_ui), op=ALU.mult), [pB_, cst], [Arb])
                                V(lambda e: e.tensor_tensor(out=Aak.t[:], in0=v3(pB_.t[:, 256:512]), in1=bc2(m_us), op=ALU.mult), [pB_, cst], [Aak])
                                V(lambda e: e.tensor_tensor(out=Ark.t[:], in0=v3(pC_.t[:, 0:256]), in1=bc2(m_ui), op=ALU.mult), [pC_, cst], [Ark])
                                G(lambda e: e.tensor_tensor(out=Yb[0].t[:], in0=Qm.t[:], in1=bc2(identf), op=ALU.add), [Qm, cst], [Yb[0]])
                                B["state"] = (Qm, Nm, Yb[0])
                            for lev in range(1, 7):
                                pLs = []
                                for c in range(2):
                                    B = CS[c]
                                    qc, ncur, yc = B["state"]
                                    pL = pget("c")
                                    pLs.append(pL)
                                    for h2 in range(2):
                                        MM(pL.t[:, 256 + h2 * 128:256 + (h2 + 1) * 128], qc.t[:, h2, :], ncur.t[:, h2, :], True, True, [qc, ncur], [pL])
                                        if lev < 6:
                                            MM(pL.t[:, h2 * 128:(h2 + 1) * 128], ncur.t[:, h2, :], qc.t[:, h2, :], True, True, [qc, ncur], [pL])
                                news = []
                                for c in range(2):
                                    B = CS[c]
                                    pL = pLs[c]
                                    nq = B["Qp"][lev % 2]
                                    nn = B["Np"][lev % 2]
                                    if lev < 6:
                                        A(lambda e, pL=pL, nq=nq: e.copy(nq.t[:].rearrange("p h t -> p (h t)"), pL.t[:, 0:256]), [pL], [nq])
                                    A(lambda e, pL=pL, nn=nn: e.copy(nn.t[:].rearrange("p h t -> p (h t)"), pL.t[:, 256:512]), [pL], [nn])
                                    news.append((nq, nn))
                                pYs = []
                                for c in range(2):
                                    B = CS[c]
                                    yc = B["state"][2]
                                    nn = news[c][1]
                                    pY = pget("c")
                                    pYs.append(pY)
                                    for h2 in range(2):
                                        MM(pY.t[:, h2 * 128:(h2 + 1) * 128], nn.t[:, h2, :], yc.t[:, h2, :], True, True, [nn, yc], [pY])
                                for c in range(2):
                                    B = CS[c]
                                    yc = B["state"][2]
                                    pY = pYs[c]
                                    ny = B["Yb"][lev % 2] if lev < 6 else B["Ybf"]
                                    V(lambda e, pY=pY, ny=ny, yc=yc: e.tensor_tensor(out=ny.t[:], in0=v3(pY.t[:, 0:256]), in1=yc.t[:], op=ALU.add), [pY, yc], [ny])
                                    B["state"] = (news[c][0], news[c][1], ny)
                            for c in range(2):
                                cs = slice(c * 128, (c + 1) * 128)
                                B = CS[c]
                                Arb, Aak, Ark = B["Arb"], B["Aak"], B["Ark"]
                                yc = B["state"][2]
                                hb = Hbf[c % 2]
                                A(lambda e, hb=hb: e.copy(hb.t[0:64, 0:64], H.t[0:64, :]), [H], [hb])
                                A(lambda e, hb=hb: e.copy(hb.t[64:128, 64:128], H.t[64:128, :]), [H], [hb])
                                pW = pget("c")
                                for h2 in range(2):
                                    ps_ = slice(h2 * 64, (h2 + 1) * 64)
                                    MM(pW.t[:, h2 * 64:(h2 + 1) * 64], opA.t[:, cs], hb.t[:, h2 * 64:(h2 + 1) * 64], True, False, [opA, hb], [pW])
                                    MM(pW.t[:, h2 * 64:(h2 + 1) * 64], Aak.t[:, h2, :], vtok.t[:, c, h2 * 64:(h2 + 1) * 64], False, True, [Aak, vtok], [pW])
                                A(lambda e, pW=pW: e.copy(W0sb.t[:], pW.t[:, 0:128]), [pW], [W0sb])
                                pU = pget("c")
                                for h2 in range(2):
                                    MM(pU.t[:, h2 * 64:(h2 + 1) * 64], yc.t[:, h2, :], W0sb.t[:, h2 * 64:(h2 + 1) * 64], True, True, [yc, W0sb], [pU])
                                A(lambda e, pU=pU: e.copy(Usb.t[:], pU.t[:, 0:128]), [pU], [Usb])
                                for h2 in range(2):
                                    ps_ = slice(h2 * 64, (h2 + 1) * 64)
                                    MM(pO.t[ps_, cs], hb.t[:, h2 * 64:(h2 + 1) * 64], opR.t[:, cs], True, False, [hb, opR], [pO])
                                    MM(pO.t[ps_, cs], Usb.t[:, h2 * 64:(h2 + 1) * 64], Arb.t[:, h2, :], False, False, [Usb, Arb], [pO])
                                    MM(pO.t[ps_, cs], vtok.t[:, c, h2 * 64:(h2 + 1) * 64], Ark.t[:, h2, :], False, True, [vtok, Ark], [pO])
                                pH = pget("c")
                                for h2 in range(2):
                                    ps_ = slice(h2 * 64, (h2 + 1) * 64)
                                    MM(pH.t[ps_, 0:64], btok.t[:, c, ps_], Usb.t[:, ps_], True, False, [btok, Usb], [pH])
                                    MM(pH.t[ps_, 0:64], ktok.t[:, c, ps_], vtok.t[:, c, ps_], False, True, [ktok, vtok], [pH])
                                V(lambda e, pH=pH: e.tensor_tensor(out=H.t[:], in0=pH.t[:, 0:64], in1=H.t[:], op=ALU.add), [pH, H], [H])
                                gcol = Ep.t[:, c * 128 + 127:c * 128 + 128]
                                V(lambda e, gcol=gcol: e.tensor_scalar(out=H.t[:], in0=H.t[:], scalar1=gcol, scalar2=None, op0=ALU.mult), [H, Ep], [H])
                            if last:
                                pt = pget("m")
                                TR(pt.t[0:64, 0:128], H.t[:, :], identf, [H, cst], [pt])
                                ho = fget()
                                A(lambda e: e.copy(ho.t[0:64, 0:128], pt.t[0:64, 0:128]), [pt], [ho])
                                DMA(o_wkv_p[l, s, 2 * hp:2 * hp + 2].rearrange("h v k -> v h k"),
                                    ho.t[0:64, 0:128].rearrange("p (h k) -> p h k", h=2), reads=[ho], writes=[])
                            stop_at('r4')
                            Osb = fget()
                            A(lambda e: e.copy(Osb.t[:], pO.t[:, 0:TB]), [pO], [Osb])
                            Ob = bget()
                            V(lambda e: e.tensor_copy(Ob.t[:], pO.t[:, 0:TB]), [pO], [Ob])
                            pm = pget("m")
                            MM(pm.t[:, 0:TB], bo64_b.t[:], Ob.t[:], True, True, [bo64_b, Ob], [pm])
                            dd = fget()
                            V(lambda e: e.tensor_tensor(out=dd.t[:], in0=Osb.t[:], in1=pm.t[:, 0:TB], op=ALU.subtract), [Osb, pm], [dd])
                            dq = bget()
                            A(lambda e: e.activation(out=dq.t[:], in_=dd.t[:], func=AF.Square), [dd], [dq])
                            pv_ = pget("m")
                            MM(pv_.t[:, 0:TB], bo64_b.t[:], dq.t[:], True, True, [bo64_b, dq], [pv_])
                            sd = fget()
                            A(lambda e: e.activation(out=sd.t[:], in_=pv_.t[:, 0:TB], func=AF.Ln, bias=64e-5, scale=1.0), [pv_], [sd])
                            A(lambda e: e.activation(out=sd.t[:], in_=sd.t[:], func=AF.Exp, scale=-0.5), [sd], [sd])
                            G(lambda e: e.tensor_tensor(out=dd.t[:], in0=dd.t[:], in1=sd.t[:], op=ALU.mult), [dd, sd], [dd])
                            G(lambda e: e.tensor_scalar(out=dd.t[:], in0=dd.t[:], scalar1=pc(l, 41 + hp), scalar2=pc(l, 45 + hp), op0=ALU.mult, op1=ALU.add), [dd, pcol], [dd])
                            G(lambda e: e.tensor_tensor(out=dd.t[:], in0=dd.t[:], in1=bon.t[:], op=ALU.add), [dd, bon], [dd])
                            pz = proj(C_ZA + hp * 128)
                            zs = fget()
                            A(lambda e: e.activation(out=zs.t[:], in_=pz.t[:, 0:TB], func=AF.Silu), [pz], [zs])
                            V(lambda e: e.tensor_tensor(out=yaT.t[:, hp, :], in0=dd.t[:], in1=zs.t[:], op=ALU.mult), [dd, zs], [yaT])

                        stop_at('rwkv')
                        stop_at(f'rwkv{blk}')
                        def qknorm(pb, gcol, outs):
                            sq = bget()
                            A(lambda e: e.activation(out=sq.t[:], in_=pb.t[:, 0:TB], func=AF.Square), [pb], [sq])
                            pmm = pget("m")
                            MM(pmm.t[:, 0:TB], bo64_b.t[:], sq.t[:], True, True, [bo64_b, sq], [pmm])
                            rs = fget()
                            A(lambda e: e.activation(out=rs.t[:], in_=pmm.t[:, 0:TB], func=AF.Ln, bias=1e-6, scale=1.0), [pmm], [rs])
                            A(lambda e: e.activation(out=rs.t[:], in_=rs.t[:], func=AF.Exp, scale=-0.5), [rs], [rs])
                            for (oap, ot, prt) in outs:
                                V(lambda e, oap=oap, prt=prt: e.scalar_tensor_tensor(out=oap, in0=pb.t[prt, 0:TB], scalar=gcol[prt, :], in1=rs.t[prt, :],
                                                                                     op0=ALU.mult, op1=ALU.mult), [pb, rs, pcol, dcol], [ot])

                        kf = fget()
                        allp = slice(0, 128)
                        for g in range(2):
                            pb = proj(C_KD + g * 128)
                            outs = [(knT.t[:, g, 128:128 + TB], knT, allp)]
                            if last:
                                outs.append((kf.t[g * 64:(g + 1) * 64, 0:128], kf, slice(g * 64, (g + 1) * 64)))
                            if last:
                                qknorm(pb, pc(l, 50), [outs[0]])
                                sqf = bget()
                                A(lambda e, pb=pb, sqf=sqf: e.activation(out=sqf.t[:, 0:128], in_=pb.t[:, TB - 128:TB], func=AF.Square), [pb], [sqf])
                                pmm = pget("m")
                                MM(pmm.t[:, 0:128], bo64_b.t[:], sqf.t[:, 0:128], True, True, [bo64_b, sqf], [pmm])
                                rsf = fget()
                                A(lambda e, pmm=pmm, rsf=rsf: e.activation(out=rsf.t[:, 0:128], in_=pmm.t[:, 0:128], func=AF.Ln, bias=1e-6, scale=1.0), [pmm], [rsf])
                                A(lambda e, rsf=rsf: e.activation(out=rsf.t[:, 0:128], in_=rsf.t[:, 0:128], func=AF.Exp, scale=-0.5), [rsf], [rsf])
                                gs = slice(g * 64, (g + 1) * 64)
                                V(lambda e, pb=pb, rsf=rsf, gs=gs: e.scalar_tensor_tensor(out=kf.t[gs, 0:128], in0=pb.t[gs, TB - 128:TB], scalar=pc(l, 50)[gs, :],
                                                                                         in1=rsf.t[gs, 0:128], op0=ALU.mult, op1=ALU.mult), [pb, rsf, pcol], [kf])
                            else:
                                qknorm(pb, pc(l, 50), outs)
                        if last:
                            pt = pget("m")
                            TR(pt.t[:, 0:128], kf.t[:, 0:128], identf, [kf, cst], [pt])
                            ko = fget()
                            A(lambda e, pt=pt, ko=ko: e.copy(ko.t[:, 0:128], pt.t[:, 0:128]), [pt], [ko])
                            DMA(o_k_p[l, s], ko.t[:, 0:128], reads=[ko], writes=[])
                            out_res.append(ko.r)
                        stop_at(f'k{blk}')
                        for hp in range(4):
                            pb = proj(C_Q + hp * 128)
                            qknorm(pb, dcol.t[:, 4:5], [(qnT.t[:, hp, :], qnT, allp)])
                        pb = proj(C_VB)
                        vf = fget()
                        A(lambda e: e.copy(vf.t[:], pb.t[:, 0:TB]), [pb], [vf])
                        pt = pget("m")
                        for ti in range(2):
                            TR(pt.t[:, ti * 128:(ti + 1) * 128], vf.t[:, ti * 128:(ti + 1) * 128], identf, [vf, cst], [pt])
                        A(lambda e, pt=pt: e.copy(vbtok.t[:, 1:3, :].rearrange("p c t -> p (c t)"), pt.t[:, 0:256]), [pt], [vbtok])
                        if last:
                            vo = fget()
                            A(lambda e, pt=pt, vo=vo: e.copy(vo.t[:, 0:128], pt.t[:, 128:256]), [pt], [vo])
                            DMA(o_v_p[l, s], vo.t[:, 0:128], reads=[vo], writes=[])
                            out_res.append(vo.r)
                        stop_at(f'v{blk}')
                        for hp in range(4):
                            g = hp // 2
                            DMA(BMt.t[:], bm_d[:, hp, :], writes=[BMt])
                            pOb = banks[6]
                            pDn = banks[7]
                            def _scores(ti_):
                                first_ = (blk == 0 and ti_ == 0)
                                psc_ = pget("c")
                                for h2 in range(2):
                                    ps_ = slice(h2 * 64, (h2 + 1) * 64)
                                    for hf in ([1] if first_ else [0, 1]):
                                        kcols = slice((ti_ + hf) * 128, (ti_ + hf + 1) * 128)
                                        MM(psc_.t[:, (h2 * 2 + hf) * 128:(h2 * 2 + hf + 1) * 128], knT.t[ps_, g, kcols], qnT.t[ps_, hp, ti_ * 128:(ti_ + 1) * 128],
                                           True, True, [knT, qnT], [psc_])
                                return psc_

                            nxt_psc = _scores(0)
                            for ti in range(2):
                                first = (blk == 0 and ti == 0)
                                halves = [1] if first else [0, 1]
                                psc = nxt_psc
                                if ti + 1 < 2:
                                    nxt_psc = _scores(ti + 1)
                                sc = fget()
                                scb = fget()
                                Pt = bget()
                                Pb = bget()
                                for h2 in range(2):
                                    dst = sc if h2 == 0 else scb
                                    pdst = Pt if h2 == 0 else Pb
                                    lo = 128 if first else 0
                                    V(lambda e, dst=dst, h2=h2, lo=lo, psc=psc: e.tensor_tensor(out=dst.t[:, lo:256], in0=psc.t[:, h2 * 256 + lo:(h2 + 1) * 256],
                                                                                               in1=BMt.t[:, h2 * 256 + lo:(h2 + 1) * 256], op=ALU.add), [psc, BMt], [dst])
                                    A(lambda e, dst=dst, pdst=pdst, lo=lo: e.activation(out=pdst.t[:, lo:256], in_=dst.t[:, lo:256], func=AF.Exp), [dst], [pdst])
                                    ps_ = slice(h2 * 64, (h2 + 1) * 64)
                                    oc = slice(ti * 128, (ti + 1) * 128)
                                    for i, hf in enumerate(halves):
                                        st, sp = i == 0, i == len(halves) - 1
                                        MM(pOb.t[ps_, oc], vbtok.t[:, ti + hf, g * 64:(g + 1) * 64], pdst.t[:, hf * 128:(hf + 1) * 128], st, sp, [vbtok, pdst], [pOb])
                                    for i, hf in enumerate(halves):
                                        st, sp = i == 0, i == len(halves) - 1
                                        MM(pDn.t[ps_, oc], ones_b.t[:, :], pdst.t[:, hf * 128:(hf + 1) * 128], st, sp, [ones_b, pdst], [pDn])
                            dn = fget()
                            V(lambda e, pDn=pDn, dn=dn: e.tensor_scalar(out=dn.t[:], in0=pDn.t[:, 0:TB], scalar1=dcol.t[:, 5 + hp:6 + hp], scalar2=None, op0=ALU.add), [pDn, dcol], [dn])
                            V(lambda e, dn=dn: e.reciprocal(dn.t[:], dn.t[:]), [dn], [dn])
                            V(lambda e, pOb=pOb, dn=dn: e.tensor_tensor(out=dn.t[:], in0=pOb.t[:, 0:TB], in1=dn.t[:], op=ALU.mult), [pOb, dn], [dn])
                            pz = proj(C_ZB + hp * 128)
                            zs = fget()
                            A(lambda e, pz=pz, zs=zs: e.activation(out=zs.t[:], in_=pz.t[:, 0:TB], func=AF.Silu), [pz], [zs])
                            V(lambda e, dn=dn, zs=zs, hp=hp: e.tensor_tensor(out=ybT.t[:, hp, :], in0=dn.t[:], in1=zs.t[:], op=ALU.mult), [dn, zs], [ybT])
                        G(lambda e: e.tensor_copy(knT.t[:, :, 0:128], knT.t[:, :, TB:TB + 128]), [knT], [knT])
                        G(lambda e: e.tensor_copy(vbtok.t[:, 0, :], vbtok.t[:, 2, :]), [vbtok], [vbtok])

                        stop_at('attn')
                        stop_at(f'attn{blk}')
                        for n in range(8):
                            pa = pget("c")
                            pbb = pget("c")
                            for c in range(4):
                                MM(pa.t[:, 0:TB], Woa.t[:, c, n * 128:(n + 1) * 128], yaT.t[:, c, :], c == 0, c == 3, [Woa, yaT], [pa])
                            for c in range(4):
                                MM(pbb.t[:, 0:TB], Wob.t[:, c, n * 128:(n + 1) * 128], ybT.t[:, c, :], c == 0, c == 3, [Wob, ybT], [pbb])
                            pga = proj(C_GA + n * 128)
                            sga = fget()
                            A(lambda e, pga=pga, sga=sga: e.activation(out=sga.t[:], in_=pga.t[:, 0:TB], func=AF.Sigmoid), [pga], [sga])
                            pgb = proj(C_GB + n * 128)
                            sgb = fget()
                            A(lambda e, pgb=pgb, sgb=sgb: e.activation(out=sgb.t[:], in_=pgb.t[:, 0:TB], func=AF.Sigmoid), [pgb], [sgb])
                            V(lambda e, pa=pa, sga=sga: e.tensor_tensor(out=sga.t[:], in0=pa.t[:, 0:TB], in1=sga.t[:], op=ALU.mult), [pa, sga], [sga])
                            V(lambda e, pbb=pbb, sgb=sgb: e.tensor_tensor(out=sgb.t[:], in0=pbb.t[:, 0:TB], in1=sgb.t[:], op=ALU.mult), [pbb, sgb], [sgb])
                            G(lambda e, sga=sga, sgb=sgb, n=n: e.tensor_tensor(out=mT.t[:, n, :], in0=sga.t[:], in1=sgb.t[:], op=ALU.add), [sga, sgb], [mT])

                        stop_at('merge')
                        stop_at(f'merge{blk}')
                        for ti in range(TB // 128):
                            xt = xts[ti % 2]
                            DMA(xt.t[:], xsrc[s, t0 + ti * 128:t0 + (ti + 1) * 128, :], reads=rd_x, writes=[xt])
                            for hf in range(2):
                                pf = pget("p")
                                for kc in range(8):
                                    MM(pf.t[:, :], mT.t[:, kc, ti * 128:(ti + 1) * 128], Wout.t[:, kc, hf * 512:(hf + 1) * 512], kc == 0, kc == 7, [mT, Wout], [pf])
                                o1 = fget()
                                o2 = fget()
                                hsl = slice(hf * 512, (hf + 1) * 512)
                                for q2, ot in enumerate((o1, o2)):
                                    cs2 = slice(hf * 512 + q2 * 256, hf * 512 + (q2 + 1) * 256)
                                    V(lambda e, pf=pf, ot=ot, q2=q2, cs2=cs2: e.tensor_tensor(out=ot.t[:], in0=pf.t[:, q2 * 256:(q2 + 1) * 256], in1=gate_bc.t[:, cs2], op=ALU.mult), [pf, gate_bc], [ot])
                                    G(lambda e, ot=ot, cs2=cs2: e.tensor_tensor(out=ot.t[:], in0=ot.t[:], in1=xt.t[:, cs2], op=ALU.add), [ot, xt], [ot])
                                    wr = [r_xmid[s][blk]] if l == 0 else []
                                    DMA(xdst[s, t0 + ti * 128:t0 + (ti + 1) * 128, cs2], ot.t[:], reads=[ot], writes=wr)
                                    if l == 1:
                                        out_res.append(ot.r)
                            if ti == 0 and blk + 1 < NBLK:
                                rdn = [r_xmid[s][blk + 1]] if l == 1 else []
                                DMA(xts[0].t[:], xsrc[s, t0 + TB:t0 + TB + 128, :], reads=rdn, writes=[xts[0]])
                                xpf[0] = True
                        if last:
                            pt = pget("m")
                            TR(pt.t[0:13, 0:128], carry.t[:, 0:13], identf, [carry, cst], [pt])
                            so = fget()
                            A(lambda e: e.copy(so.t[0:13, 0:128], pt.t[0:13, 0:128]), [pt], [so])
                            DMA(o_sh_p[l, s].rearrange("(c p) -> c p", p=128), so.t[0:13, 0:128], reads=[so], writes=[])
                        stop_at(f'blk{blk}')
                S.barrier()
                off[0] = base_off
                stop_at(f"prompt{l}")

                xn = sb([NS, D])
                gts = sb([NS, D])
                DMA(gts.t[:], gate_d[l, 0:NS, :], reads=[r_gd[l]], writes=[gts])
                xs_res = sb([NS, D])
                DMA(xs_res.t[:], xs_in if l == 0 else xs_mid, reads=[r_xsmid], writes=[xs_res])
                hTs = sb([128, 8, NS], BF16)
                us = sb([128, 43, NS])
                ua = sb([64, 20, NS])
                prevT = sb([NS, SHIFT_W])
                xsh = sb([128, 13, NS])
                sho = prevT
                Xv = sb([128, 5, 4, NS])
                vT = sb([128, 4, NS])
                bonS = sb([128, 4, NS])
                S0 = sb([128, 8, 4, 64])
                Dx = sb([128, 5, 512])
                tmpS = [sb([128, 512]), sb([128, 512])]
                sa = sb([128, 2, 4])
                oS = sb([128, NS, 4])
                cK = sb([128, NS, 128])
                cVap = S0.t[:].rearrange("p a b c -> p (a b c)").rearrange("p (b n) -> p b n", n=128)
                st = [sb([128, 4 * NS]) for _ in range(8)]
                h0s = sb([128, 8 * NS])
                sb16 = [sb([128, 4 * NS], BF16) for _ in range(3)]
                yaS = sb([128, 4, NS], BF16)
                ybS = sb([128, 4, NS], BF16)
                yb64 = sb([64, 8, NS], BF16)
                mS = sb([128, 8, NS], BF16)
                qn64 = sb([64, 10, NS])
                Dq = Dx.t[0:64, 0:4, :]
                scS = sb([128, NS, 8])
                pS = sb([128, NS, 8])
                at = [sb([64, 8, NS]) for _ in range(6)]
                kvo = sb([32, 2, 64])

                DMA(cK.t[:], ca_k[l].rearrange("b p n -> p b n"), writes=[cK])
                DMA(prevT.t[:], st_shift[l], writes=[prevT])
                V(lambda e: e.memset(ssq.t[0:NS, 0:1], 0.0), [], [ssq])
                A(lambda e: e.activation(out=xn.t[:], in_=xs_res.t[:], func=AF.Square, accum_out=ssq.t[0:NS, 0:1]), [xs_res, ssq], [xn, ssq])
                A(lambda e: e.activation(out=ssq.t[0:NS, 1:2], in_=ssq.t[0:NS, 0:1], func=AF.Sqrt, bias=1e-6, scale=1.0 / D), [ssq], [ssq])
                V(lambda e: e.reciprocal(ssq.t[0:NS, 2:3], ssq.t[0:NS, 1:2]), [ssq], [ssq])
                V(lambda e: e.tensor_scalar(out=xn.t[:], in0=xs_res.t[:], scalar1=ssq.t[0:NS, 2:3], scalar2=None, op0=ALU.mult), [xs_res, ssq], [xn])
                pb = pget("m")
                for kc in range(8):
                    TR(pb.t[:, kc * NS:(kc + 1) * NS], xn.t[0:NS, kc * 128:(kc + 1) * 128], identf[0:NS, 0:NS], [xn, cst], [pb])
                h0 = h0s
                v8 = lambda ap: ap.rearrange("p (c b) -> p c b", b=NS)
                V(lambda e: e.tensor_tensor(out=v8(h0.t[:, 0:8 * NS]), in0=v8(pb.t[:, 0:8 * NS]), in1=AT.t[:, :, 0:NS], op=ALU.mult), [pb, AT], [h0])
                V(lambda e: e.tensor_tensor(out=hTs.t[:], in0=v8(h0.t[:, 0:8 * NS]), in1=modT.t[:, 0:8, 0:NS], op=ALU.add), [h0, modT], [hTs])
                for grp in (list(range(0, 17)), list(range(27, 43))):
                    pb = pget("p")
                    for i, j in enumerate(grp):
                        for kc in range(8):
                            MM(pb.t[:, i * NS:(i + 1) * NS], Win.t[:, kc, j * 128:(j + 1) * 128], hTs.t[:, kc, :], kc == 0, kc == 7, [Win, hTs], [pb])
                    A(lambda e, pb=pb, grp=grp: e.copy(us.t[:, grp[0]:grp[-1] + 1, :].rearrange("p c b -> p (c b)"), pb.t[:, 0:len(grp) * NS]), [pb], [us])
                pb = pget("p")
                cols64 = [C_Q + h * 64 for h in range(8)] + [C_KB, C_KB + 64, C_VB, C_VB + 64] + [C_ZB + h * 64 for h in range(8)]
                for i, c0 in enumerate(cols64):
                    for kc in range(8):
                        MM(pb.t[0:64, i * NS:(i + 1) * NS], Win.t[:, kc, c0:c0 + 64], hTs.t[:, kc, :], kc == 0, kc == 7, [Win, hTs], [pb])
                A(lambda e, pb=pb: e.copy(ua.t[:].rearrange("p c b -> p (c b)"), pb.t[0:64, 0:20 * NS]), [pb], [ua])
                if l + 1 < DEPTH:
                    load_weights(l + 1, "win")
                pb = pget("m")
                for j in range(13):
                    TR(pb.t[:, j * NS:(j + 1) * NS], prevT.t[0:NS, j * 128:(j + 1) * 128], identf[0:NS, 0:NS], [prevT, cst], [pb])
                v13 = lambda ap: ap.rearrange("p (c b) -> p c b", b=NS)
                V(lambda e: e.tensor_tensor(out=xsh.t[:], in0=v13(pb.t[:, 0:13 * NS]), in1=us.t[:, 0:13, :], op=ALU.subtract), [pb, us], [xsh])
                V(lambda e: e.tensor_tensor(out=xsh.t[:], in0=xsh.t[:], in1=pcol.t[:, l, 8:21].unsqueeze(2).to_broadcast([128, 13, NS]), op=ALU.mult), [xsh, pcol], [xsh])
                V(lambda e: e.tensor_tensor(out=xsh.t[:], in0=xsh.t[:], in1=us.t[:, 0:13, :], op=ALU.add), [xsh, us], [xsh])
                for q4 in range(4):
                    n = 4 if q4 < 3 else 1
                    pb = pget("m")
                    for i in range(n):
                        j = q4 * 4 + i
                        TR(pb.t[0:NS, i * 128:(i + 1) * 128], us.t[:, j, :], identf, [us, cst], [pb])
                    A(lambda e, pb=pb, q4=q4, n=n: e.copy(sho.t[:, q4 * 512:q4 * 512 + n * 128], pb.t[0:NS, 0:n * 128]), [pb], [sho])
                DMA(o_sh_s[l], sho.t[:], reads=[sho], writes=[])
                out_res.append(sho.r)
                tw = sb16[0]
                A(lambda e: e.activation(out=tw.t[0:64, 0:NS], in_=xsh.t[0:64, 12, :], func=AF.Tanh), [xsh], [tw])
                V(lambda e: e.tensor_copy(tw.t[64:128, 0:NS], xsh.t[64:128, 12, :]), [xsh], [tw])
                pw = pget("m")
                for hp in range(4):
                    MM(pw.t[:, hp * NS:(hp + 1) * NS], Wlo.t[0:64, hp * 128:(hp + 1) * 128], tw.t[0:64, 0:NS], True, True, [Wlo, tw], [pw])
                    MM(pw.t[:, 64 + hp * NS:64 + (hp + 1) * NS], Wlo.t[64:128, hp * 128:(hp + 1) * 128], tw.t[64:128, 0:NS], True, True, [Wlo, tw], [pw])
                v4 = lambda ap: ap.rearrange("p (c b) -> p c b", b=NS)
                colb = lambda j0: pcol.t[:, l, j0:j0 + 4].unsqueeze(2).to_broadcast([128, 4, NS])
                sgS, agS, krS, kkS, kmS, t5, t6, t7 = st
                V(lambda e: e.tensor_tensor(out=v4(sgS.t[:]), in0=v4(pw.t[:, 0:64]), in1=colb(21), op=ALU.add), [pw, pcol], [sgS])
                V(lambda e: e.tensor_tensor(out=v4(agS.t[:]), in0=v4(pw.t[:, 64:128]), in1=colb(25), op=ALU.add), [pw, pcol], [agS])
                A(lambda e: e.activation(out=sgS.t[:], in_=sgS.t[:], func=AF.Sigmoid), [sgS], [sgS])
                A(lambda e: e.activation(out=agS.t[:], in_=agS.t[:], func=AF.Sigmoid), [agS], [agS])
                A(lambda e: e.activation(out=Xv.t[:, 1, :, :], in_=v4(sgS.t[:]), func=AF.Exp, scale=-DC), [sgS], [Xv])
                V(lambda e: e.tensor_tensor(out=v4(krS.t[:]), in0=xsh.t[:, 4:8, :], in1=colb(29), op=ALU.mult), [xsh, pcol], [krS])
                V(lambda e: e.tensor_tensor(out=t5.t[:], in0=krS.t[:], in1=krS.t[:], op=ALU.mult), [krS], [t5])
                pn = pget("m")
                MM(pn.t[:, 0:64], bo_f, t5.t[:], True, True, [cst, t5], [pn])
                A(lambda e: e.activation(out=t6.t[:], in_=pn.t[:, 0:64], func=AF.Sqrt), [pn], [t6])
                V(lambda e: e.tensor_scalar(out=t6.t[:], in0=t6.t[:], scalar1=1e-12, scalar2=None, op0=ALU.max), [t6], [t6])
                V(lambda e: e.reciprocal(t6.t[:], t6.t[:]), [t6], [t6])
                V(lambda e: e.tensor_tensor(out=kkS.t[:], in0=krS.t[:], in1=t6.t[:], op=ALU.mult), [krS, t6], [kkS])
                V(lambda e: e.tensor_scalar(out=Xv.t[:, 0, :, :], in0=v4(kkS.t[:]), scalar1=-1.0, scalar2=None, op0=ALU.mult), [kkS], [Xv])
                V(lambda e: e.tensor_tensor(out=Xv.t[:, 2, :, :], in0=v4(kkS.t[:]), in1=v4(agS.t[:]), op=ALU.mult), [kkS, agS], [Xv])
                V(lambda e: e.tensor_tensor(out=v4(kmS.t[:]), in0=v4(agS.t[:]), in1=colb(33), op=ALU.mult), [agS, pcol], [kmS])
                V(lambda e: e.tensor_tensor(out=v4(kmS.t[:]), in0=v4(kmS.t[:]), in1=dcol.t[:, 0:4].unsqueeze(2).to_broadcast([128, 4, NS]), op=ALU.add), [kmS, dcol], [kmS])
                V(lambda e: e.tensor_tensor(out=Xv.t[:, 3, :, :], in0=v4(kmS.t[:]), in1=xsh.t[:, 4:8, :], op=ALU.mult), [kmS, xsh], [Xv])
                V(lambda e: e.tensor_copy(Xv.t[:, 4, :, :], xsh.t[:, 0:4, :]), [xsh], [Xv])
                V(lambda e: e.tensor_copy(vT.t[:], xsh.t[:, 8:12, :]), [xsh], [vT])
                V(lambda e: e.tensor_tensor(out=v4(t5.t[:]), in0=xsh.t[:, 0:4, :], in1=colb(37), op=ALU.mult), [xsh, pcol], [t5])
                V(lambda e: e.tensor_tensor(out=v4(t5.t[:]), in0=v4(t5.t[:]), in1=Xv.t[:, 3, :, :], op=ALU.mult), [t5, Xv], [t5])
                pbn = pget("m")
                MM(pbn.t[:, 0:64], bo_f, t5.t[:], True, True, [cst, t5], [pbn])
                V(lambda e: e.tensor_tensor(out=bonS.t[:], in0=v4(pbn.t[:, 0:64]), in1=vT.t[:], op=ALU.mult), [pbn, vT], [bonS])
                for bp in range(NS // 2):
                    b0 = bp * 2
                    bl = b0 % 8
                    if bl == 0:
                        DMA(S0.t[:], st_wkv[l, b0:b0 + 8].rearrange("b (hp h2) v k -> (h2 v) b hp k", h2=2), writes=[S0])
                    for i in range(5):
                        V(lambda e, i=i, b0=b0: e.tensor_tensor(
                            out=Dx.t[:, i, :].rearrange("p (b hp k) -> p b hp k", b=2, hp=4),
                            in0=ident2.unsqueeze(1).unsqueeze(1).to_broadcast([128, 2, 4, 64]),
                            in1=Xv.t[:, i, :, b0:b0 + 2].rearrange("p hp b -> p b hp").unsqueeze(3).to_broadcast([128, 2, 4, 64]),
                            op=ALU.mult), [cst, Xv], [Dx])
                    pbs = []
                    for i in range(5):
                        pbk = banks[i]
                        MM(pbk.t[:, :], bo_f, Dx.t[:, i, :], True, True, [cst, Dx], [pbk])
                        pbs.append(pbk)
                    Sv = S0.t[:, bl:bl + 2, :, :].rearrange("p b hp k -> p (b hp k)")
                    t1, t2 = tmpS
                    V(lambda e, Sv=Sv: e.tensor_tensor(out=t1.t[:], in0=Sv, in1=pbs[0].t[:, :], op=ALU.mult), [S0, pbs[0]], [t1])
                    V(lambda e: e.tensor_reduce(out=sa.t[:].rearrange("p a b -> p (a b)"),
                                                in_=t1.t[:].rearrange("p (g k) -> p g k", k=64), axis=AX.X, op=ALU.add), [t1], [sa])
                    V(lambda e, Sv=Sv: e.tensor_tensor(out=Sv, in0=Sv, in1=pbs[1].t[:, :], op=ALU.mult), [S0, pbs[1]], [S0])
                    V(lambda e: e.tensor_tensor(out=t1.t[:].rearrange("p (g k) -> p g k", k=64), in0=pbs[2].t[:, :].rearrange("p (g k) -> p g k", k=64),
                                                in1=sa.t[:].rearrange("p a b -> p (a b)").unsqueeze(2).to_broadcast([128, 8, 64]), op=ALU.mult), [pbs[2], sa], [t1])
                    V(lambda e, Sv=Sv: e.tensor_tensor(out=Sv, in0=Sv, in1=t1.t[:], op=ALU.add), [S0, t1], [S0])
                    V(lambda e, b0=b0: e.tensor_tensor(out=t2.t[:].rearrange("p (b hp k) -> p b hp k", b=2, hp=4),
                                                       in0=pbs[3].t[:, :].rearrange("p (b hp k) -> p b hp k", b=2, hp=4),
                                                       in1=vT.t[:, :, b0:b0 + 2].rearrange("p hp b -> p b hp").unsqueeze(3).to_broadcast([128, 2, 4, 64]),
                                                       op=ALU.mult), [pbs[3], vT], [t2])
                    V(lambda e, Sv=Sv: e.tensor_tensor(out=Sv, in0=Sv, in1=t2.t[:], op=ALU.add), [S0, t2], [S0])
                    V(lambda e, Sv=Sv: e.tensor_tensor(out=t1.t[:], in0=Sv, in1=pbs[4].t[:, :], op=ALU.mult), [S0, pbs[4]], [t1])
                    V(lambda e, b0=b0: e.tensor_reduce(out=oS.t[:, b0:b0 + 2, :].rearrange("p b hp -> p (b hp)"),
                                                       in_=t1.t[:].rearrange("p (g k) -> p g k", k=64), axis=AX.X, op=ALU.add), [t1], [oS])
                    if bl == 6:
                        DMA(o_wkv_s[l, b0 - 6:b0 + 2].rearrange("b (hp h2) v k -> (h2 v) b hp k", h2=2), S0.t[:], reads=[S0], writes=[])
                DMA(cVap, ca_v[l].rearrange("b p n -> p b n"), reads=[], writes=[S0])
                oT = t5
                V(lambda e: e.tensor_copy(v4(oT.t[:]), oS.t[:].rearrange("p b hp -> p hp b")), [oS], [oT])
                pm = pget("m")
                MM(pm.t[:, 0:64], bo_f, oT.t[:], True, True, [cst, oT], [pm])
                V(lambda e: e.scalar_tensor_tensor(out=t6.t[:], in0=pm.t[:, 0:64], scalar=-1.0 / 64, in1=oT.t[:], op0=ALU.mult, op1=ALU.add), [pm, oT], [t6])
                V(lambda e: e.tensor_tensor(out=t7.t[:], in0=t6.t[:], in1=t6.t[:], op=ALU.mult), [t6], [t7])
                pv_ = pget("m")
                MM(pv_.t[:, 0:64], bo_f, t7.t[:], True, True, [cst, t7], [pv_])
                A(lambda e: e.activation(out=t7.t[:], in_=pv_.t[:, 0:64], func=AF.Sqrt, bias=64e-5, scale=1.0 / 64), [pv_], [t7])
                V(lambda e: e.reciprocal(t7.t[:], t7.t[:]), [t7], [t7])
                V(lambda e: e.tensor_tensor(out=t6.t[:], in0=t6.t[:], in1=t7.t[:], op=ALU.mult), [t6, t7], [t6])
                V(lambda e: e.tensor_tensor(out=v4(t6.t[:]), in0=v4(t6.t[:]), in1=colb(41), op=ALU.mult), [t6, pcol], [t6])
                V(lambda e: e.tensor_tensor(out=v4(t6.t[:]), in0=v4(t6.t[:]), in1=colb(45), op=ALU.add), [t6, pcol], [t6])
                V(lambda e: e.tensor_tensor(out=v4(t6.t[:]), in0=v4(t6.t[:]), in1=bonS.t[:], op=ALU.add), [t6, bonS], [t6])
                A(lambda e: e.activation(out=v4(t7.t[:]), in_=us.t[:, 13:17, :], func=AF.Silu), [us], [t7])
                V(lambda e: e.tensor_tensor(out=yaS.t[:], in0=v4(t6.t[:]), in1=v4(t7.t[:]), op=ALU.mult), [t6, t7], [yaS])

                a0_, a1_, a2_, a3_, a4_, a5_ = at
                f64 = lambda ap: ap.rearrange("p c b -> p (c b)")
                V(lambda e: e.tensor_tensor(out=qn64.t[:], in0=ua.t[:, 0:10, :], in1=ua.t[:, 0:10, :], op=ALU.mult), [ua], [qn64])
                pq = pget("m")
                MM(pq.t[0:64, 0:160], bo_f[0:64, 0:64], f64(qn64.t[:]), True, True, [cst, qn64], [pq])
                A(lambda e: e.activation(out=f64(qn64.t[:]), in_=pq.t[0:64, 0:160], func=AF.Sqrt, bias=1e-6, scale=1.0 / 64), [pq], [qn64])
                V(lambda e: e.reciprocal(qn64.t[:], qn64.t[:]), [qn64], [qn64])
                V(lambda e: e.tensor_tensor(out=qn64.t[:], in0=qn64.t[:], in1=ua.t[:, 0:10, :], op=ALU.mult), [qn64, ua], [qn64])
                V(lambda e: e.tensor_scalar(out=qn64.t[:, 0:8, :], in0=qn64.t[:, 0:8, :], scalar1=dcol.t[0:64, 4:5], scalar2=None, op0=ALU.mult), [qn64, dcol], [qn64])
                V(lambda e: e.tensor_scalar(out=qn64.t[:, 8:10, :], in0=qn64.t[:, 8:10, :], scalar1=pcol.t[0:64, l, 50:51], scalar2=None, op0=ALU.mult), [qn64, pcol], [qn64])
                V(lambda e: e.tensor_tensor(out=a0_.t[:].rearrange("p (g r) b -> p g r b", g=2), in0=qn64.t[:, 0:8, :].rearrange("p (g r) b -> p g r b", g=2),
                                            in1=qn64.t[:, 8:10, :].unsqueeze(2).to_broadcast([64, 2, 4, NS]), op=ALU.mult), [qn64], [a0_])
                pnw = pget("m")
                MM(pnw.t[0:64, 0:128], bo_f[0:64, 0:64], f64(a0_.t[:]), True, True, [cst, a0_], [pnw])
                b0c = smpc.t[0:64, 8 + DEPTH * 8:8 + DEPTH * 8 + 8]
                V(lambda e: e.tensor_tensor(out=a1_.t[:], in0=pnw.t[0:64, 0:128].rearrange("p (h b) -> p h b", b=NS),
                                            in1=b0c.unsqueeze(2).to_broadcast([64, 8, NS]), op=ALU.add), [pnw, smpc], [a1_])
                A(lambda e: e.activation(out=a1_.t[:], in_=a1_.t[:], func=AF.Exp), [a1_], [a1_])
                for b4 in range(NS // 4):
                    V(lambda e, b4=b4: e.tensor_tensor(
                        out=Dq.rearrange("p b (h d) -> p b h d", d=64),
                        in0=ident2[0:64, :].unsqueeze(1).unsqueeze(1).to_broadcast([64, 4, 8, 64]),
                        in1=qn64.t[:, 0:8, b4 * 4:(b4 + 1) * 4].rearrange("p h b -> p b h").unsqueeze(3).to_broadcast([64, 4, 8, 64]),
                        op=ALU.mult), [cst, qn64], [Dx])
                    for bi in range(4):
                        b = b4 * 4 + bi
                        pqb = pget("c")
                        MM(pqb.t[:, :], cst.t[0:64, K_ONE:K_ONE + 128], Dq[:, bi, :], True, True, [cst, Dx], [pqb])
                        t1 = tmpS[bi % 2]
                        V(lambda e, pqb=pqb, t1=t1, b=b: e.tensor_tensor(
                            out=t1.t[:].rearrange("p (g r d) -> p g r d", g=2, r=4),
                            in0=pqb.t[:, :].rearrange("p (g r d) -> p g r d", g=2, r=4),
                            in1=cK.t[:, b, :].rearrange("p (g d) -> p g d", g=2).unsqueeze(2).to_broadcast([128, 2, 4, 64]),
                            op=ALU.mult), [pqb, cK], [t1])
                        V(lambda e, t1=t1, b=b: e.tensor_reduce(out=scS.t[:, b, :], in_=t1.t[:].rearrange("p (h d) -> p h d", d=64), axis=AX.X, op=ALU.add), [t1], [scS])
                V(lambda e: e.tensor_tensor(out=scS.t[:], in0=scS.t[:], in1=smpc.t[:, 0:8].unsqueeze(1).to_broadcast([128, NS, 8]), op=ALU.add), [scS, smpc], [scS])
                A(lambda e: e.activation(out=pS.t[:], in_=scS.t[:], func=AF.Exp), [scS], [pS])
                pdn = banks[6]
                MM(pdn.t[0:64, 0:128], cst.t[:, K_ONE:K_ONE + 64], pS.t[:].rearrange("p b h -> p (b h)"), True, True, [cst, pS], [pdn])
                pov = banks[7]
                for b in range(NS):
                    for g in range(2):
                        MM(pov.t[0:64, b * 8 + g * 4:b * 8 + g * 4 + 4], cVap[:, b, g * 64:(g + 1) * 64], pS.t[:, b, g * 4:(g + 1) * 4], True, True, [S0, pS], [pov])
                bh = lambda ap: ap.rearrange("p (b h) -> p h b", h=8)
                es64 = smpc.t[0:64, 8 + l * 8:8 + l * 8 + 8]
                A(lambda e: e.activation(out=a5_.t[:, :, 0], in_=es64, func=AF.Exp), [smpc], [a5_])
                V(lambda e: e.tensor_tensor(out=a2_.t[:], in0=bh(pdn.t[0:64, 0:128]), in1=a1_.t[:], op=ALU.add), [pdn, a1_], [a2_])
                V(lambda e: e.tensor_tensor(out=a2_.t[:], in0=a2_.t[:], in1=a5_.t[:, :, 0:1].to_broadcast([64, 8, NS]), op=ALU.add), [a2_, a5_], [a2_])
                V(lambda e: e.reciprocal(a2_.t[:], a2_.t[:]), [a2_], [a2_])
                V(lambda e: e.tensor_tensor(out=a3_.t[:].rearrange("p (g r) b -> p g r b", g=2), in0=a1_.t[:].rearrange("p (g r) b -> p g r b", g=2),
                                            in1=ua.t[:, 10:12, :].unsqueeze(2).to_broadcast([64, 2, 4, NS]), op=ALU.mult), [a1_, ua], [a3_])
                V(lambda e: e.tensor_tensor(out=a3_.t[:], in0=a3_.t[:], in1=bh(pov.t[0:64, 0:128]), op=ALU.add), [a3_, pov], [a3_])
                V(lambda e: e.tensor_tensor(out=a3_.t[:], in0=a3_.t[:], in1=a2_.t[:], op=ALU.mult), [a3_, a2_], [a3_])
                A(lambda e: e.activation(out=a4_.t[:], in_=ua.t[:, 12:20, :], func=AF.Silu), [ua], [a4_])
                V(lambda e: e.tensor_tensor(out=yb64.t[:], in0=a3_.t[:], in1=a4_.t[:], op=ALU.mult), [a3_, a4_], [yb64])
                y4 = yb64.t[:].rearrange("p (hp h2) b -> p hp h2 b", h2=2)
                DMA(ybS.t[0:64, :, :], y4[:, :, 0, :], reads=[yb64], writes=[ybS])
                DMA(ybS.t[64:128, :, :], y4[:, :, 1, :], reads=[yb64], writes=[ybS])
                DMA(o_k_s[l][:, 0:127, :].rearrange("b p n -> p b n"), cK.t[1:128, :, :], reads=[cK], writes=[])
                DMA(o_v_s[l][:, 0:127, :].rearrange("b p n -> p b n"), cVap[1:128, :, :], reads=[S0], writes=[])
                out_res.append(cK.r)
                for (src_ap, dst_d, rr) in ((qn64.t[:, 8:10, :], o_k_s, qn64), (ua.t[:, 10:12, :], o_v_s, ua)):
                    pt = pget("m")
                    TR(pt.t[0:32, 0:64], src_ap.rearrange("p g b -> p (g b)"), identf[0:64, 0:64], [rr, cst], [pt])
                    ko = sb([32, 64])
                    A(lambda e, pt=pt, ko=ko: e.copy(ko.t[:], pt.t[0:32, 0:64]), [pt], [ko])
                    for g in range(2):
                        DMA(dst_d[l][:, 127, g * 64:(g + 1) * 64], ko.t[g * NS:(g + 1) * NS, :], reads=[ko], writes=[])
                    out_res.append(ko.r)
                pa = pget("c")
                pbb = pget("c")
                for n in range(8):
                    for c in range(4):
                        MM(pa.t[:, n * NS:(n + 1) * NS], Woa.t[:, c, n * 128:(n + 1) * 128], yaS.t[:, c, :], c == 0, c == 3, [Woa, yaS], [pa])
                    for c in range(4):
                        MM(pbb.t[:, n * NS:(n + 1) * NS], Wob.t[:, c, n * 128:(n + 1) * 128], ybS.t[:, c, :], c == 0, c == 3, [Wob, ybS], [pbb])
                gA, gB = tmpS
                A(lambda e: e.activation(out=gA.t[:, 0:128], in_=us.t[:, 27:35, :].rearrange("p c b -> p (c b)"), func=AF.Sigmoid), [us], [gA])
                A(lambda e: e.activation(out=gB.t[:, 0:128], in_=us.t[:, 35:43, :].rearrange("p c b -> p (c b)"), func=AF.Sigmoid), [us], [gB])
                V(lambda e: e.tensor_tensor(out=gA.t[:, 0:128], in0=gA.t[:, 0:128], in1=pa.t[:, 0:128], op=ALU.mult), [gA, pa], [gA])
                V(lambda e: e.tensor_tensor(out=gB.t[:, 0:128], in0=gB.t[:, 0:128], in1=pbb.t[:, 0:128], op=ALU.mult), [gB, pbb], [gB])
                V(lambda e: e.tensor_tensor(out=mS.t[:].rearrange("p c b -> p (c b)"), in0=gA.t[:, 0:128], in1=gB.t[:, 0:128], op=ALU.add), [gA, gB], [mS])
                for hf in range(2):
                    pf = pget("p")
                    for kc in range(8):
                        MM(pf.t[0:NS, :], mS.t[:, kc, :], Wout.t[:, kc, hf * 512:(hf + 1) * 512], kc == 0, kc == 7, [mS, Wout], [pf])
                    hsl = slice(hf * 512, (hf + 1) * 512)
                    V(lambda e, pf=pf, hsl=hsl: e.tensor_tensor(out=xn.t[:, hsl], in0=pf.t[0:NS, :], in1=gts.t[:, hsl], op=ALU.mult), [pf, gts], [xn])
                    V(lambda e, hsl=hsl: e.tensor_tensor(out=xs_res.t[:, hsl], in0=xs_res.t[:, hsl], in1=xn.t[:, hsl], op=ALU.add), [xs_res, xn], [xs_res])
                DMA(y_s if l == DEPTH - 1 else xs_mid, xs_res.t[:], reads=[xs_res], writes=[r_xsmid])
                if l + 1 < DEPTH:
                    load_weights(l + 1, "rest")
                S.barrier()

        try:
            _layers()
        except StopBuild as ex:
            print("STOP at", ex)
        S.barrier()
        S.emit()
    return nc


def _t5_bucket(dist):
    d = np.maximum(dist, 0)
    lr = np.log(np.maximum(d, 1).astype(np.float32) / 16) / np.float32(np.log(128 / 16))
    large = np.minimum(16 + (lr * 16).astype(np.int32), 31)
    return np.where(d < 16, d, large)


_NC_CACHE = {}


def kernel(x_prompt, x_sample, c_prompt, c_sample, state_wkv, state_shift, cache_k, cache_v,
           norm_g, w_ada, b_ada, w_in, mu_shift, w0, w_decay_up, a0, w_a_up, k_k, k_a, r_k,
           lnx_g, lnx_b, w_o_a, q_norm_g, k_norm_g, rel_bias, sinks, w_o_b, w_out):
    f = lambda a: np.ascontiguousarray(np.asarray(a, dtype=np.float32))
    x_prompt, x_sample, c_prompt, c_sample = f(x_prompt), f(x_sample), f(c_prompt), f(c_sample)
    state_wkv, state_shift, cache_k, cache_v = f(state_wkv), f(state_shift), f(cache_k), f(cache_v)
    rel_bias = f(rel_bias)
    sinks = f(sinks)
    pcol = np.zeros((128, DEPTH, NPC), np.float32)
    for l in range(DEPTH):
        pcol[:, l, 0:8] = f(norm_g)[l].reshape(8, 128).T
        pcol[:, l, 8:21] = f(mu_shift)[l].reshape(13, 128).T
        for j, a in enumerate((w0, a0, k_k, k_a, f(r_k).reshape(DEPTH, 512), lnx_g, lnx_b)):
            pcol[:, l, 21 + 4 * j:25 + 4 * j] = f(a)[l].reshape(4, 128).T
        pcol[:, l, 49] = np.tile(f(q_norm_g)[l], 2)
        pcol[:, l, 50] = np.tile(f(k_norm_g)[l], 2)
        pcol[:, l, 51:55] = np.repeat(sinks[l].reshape(4, 2), 64, axis=1).T
        pcol[:, l, 55:79] = f(b_ada)[l].reshape(24, 128).T
        pcol[:, l, 79:83] = np.repeat(rel_bias[0].reshape(4, 2), 64, axis=1).T
    bgate = np.ascontiguousarray(np.broadcast_to(f(b_ada)[:, 2048:3072][None], (18, DEPTH, D)))
    wlora = np.ascontiguousarray(np.concatenate([f(w_decay_up), f(w_a_up)], axis=1))
    tk = np.arange(128)[:, None]
    tq = np.arange(128)[None, :]
    bm = np.zeros((128, 4, 2, 2, 128), np.float32)
    for hf in range(2):
        dist = tq + 128 - tk if hf == 0 else tq - tk
        valid = (dist >= 0) & (dist <= 128)
        bk = _t5_bucket(dist)
        for h in range(8):
            bm[:, h // 2, h % 2, hf, :] = np.where(valid, rel_bias[bk, h], np.float32(NEGM))
    bm = np.ascontiguousarray(bm.reshape(128, 4, 512))
    smpc = np.zeros((128, 8 + DEPTH * 8 + 8), np.float32)
    smpc[:, 0:8] = rel_bias[_t5_bucket(128 - np.arange(128))]
    for l in range(DEPTH):
        smpc[:, 8 + l * 8:16 + l * 8] = sinks[l][None, :]
    smpc[:, 8 + DEPTH * 8:] = rel_bias[0][None, :]
    cst = np.zeros((128, K_END), np.float32)
    p = np.arange(128)[:, None]
    j = np.arange(128)[None, :]
    cst[:, K_ID:K_ID + 128] = (p == j)
    cst[:, K_MUS:K_MUS + 128] = (j > p)
    cst[:, K_MLS:K_MLS + 128] = (j < p)
    cst[:, K_MUI:K_MUI + 128] = (j >= p)
    sm = np.ones((128, 256), np.float32)
    sm[:, 0] = 0
    sm[:, 128] = 0
    cst[:, K_SCAN:K_SCAN + 256] = sm
    cst[:, K_BO:K_BO + 128] = (p // 64 == j // 64)
    cst[:, K_ID2:K_ID2 + 64] = (p % 64 == np.arange(64)[None, :])
    selm = np.zeros((128, 256), np.float32)
    selm[16, 0:128] = 1
    selm[17, 128:256] = 1
    cst[:, K_SEL:K_SEL + 256] = selm
    cst[:, K_ONE:K_ONE + 128] = 1.0
    return _run(locals())


def _run(v):
    f = lambda a: np.ascontiguousarray(np.asarray(a, dtype=np.float32))
    if "nc" not in _NC_CACHE:
        _NC_CACHE["nc"] = build_nc()
    nc = _NC_CACHE["nc"]
    shared = dict(w_ada=f(v["w_ada"]), w_in=f(v["w_in"]), w_o_a=f(v["w_o_a"]), w_o_b=f(v["w_o_b"]), w_out=f(v["w_out"]),
                  wlora=v["wlora"], pcol=v["pcol"], bgate=v["bgate"], bm=v["bm"], cst=v["cst"], smpc=v["smpc"])
    in_maps = []
    for c in range(NCORES):
        sl = slice(c * NS, (c + 1) * NS)
        m = dict(shared)
        m["xp"] = np.ascontiguousarray(v["x_prompt"][c * NP:(c + 1) * NP])
        m["xs"] = np.ascontiguousarray(v["x_sample"][sl, 0, :])
        m["c_all"] = np.ascontiguousarray(np.concatenate([v["c_sample"][sl], v["c_prompt"][c * NP:(c + 1) * NP]], axis=0))
        m["st_wkv"] = np.ascontiguousarray(v["state_wkv"][:, sl])
        m["st_shift"] = np.ascontiguousarray(v["state_shift"][:, sl])
        m["ca_k"] = np.ascontiguousarray(v["cache_k"][:, sl].reshape(DEPTH, NS, 128, 128))
        m["ca_v"] = np.ascontiguousarray(v["cache_v"][:, sl].reshape(DEPTH, NS, 128, 128))
        in_maps.append(m)
    res = run_bass_kernel_spmd(nc, in_maps, core_ids=list(range(NCORES)))
    R = res.results
    cat = lambda k, ax: np.concatenate([np.asarray(r[k], dtype=np.float32) for r in R], axis=ax)
    y_prompt = cat("y_p", 0)
    y_sample = cat("y_s", 0).reshape(128, 1, D)
    wkv_p = cat("o_wkv_p", 1)
    sh_p = cat("o_sh_p", 1)
    k_p = cat("o_k_p", 1).reshape(DEPTH, 16, 128, 2, 64)
    v_p = cat("o_v_p", 1).reshape(DEPTH, 16, 128, 2, 64)
    wkv_s = cat("o_wkv_s", 1)
    sh_s = cat("o_sh_s", 1)
    k_s = cat("o_k_s", 1).reshape(DEPTH, 128, 128, 2, 64)
    v_s = cat("o_v_s", 1).reshape(DEPTH, 128, 128, 2, 64)
    return (y_prompt, y_sample, wkv_p, sh_p, k_p, v_p, wkv_s, sh_s, k_s, v_s)
```
